# Optimizing a Trainium2 kernel written in Bass

```python
import math
import jax, jax.numpy as jnp
from jax import lax
import numpy as np

D_MODEL = 1024
BATCH = 8
SEQ = 2048
DEPTH = 4

CHUNK = 64
Q_BLOCK = 128

ATTN_HEADS = 4
ATTN_WIDTH = D_MODEL // 2
ATTN_HEAD_DIM = ATTN_WIDTH // (2 * ATTN_HEADS)

POOL_WINDOWS = (2, 4, 8, 16)
POOL_GROUPS = len(POOL_WINDOWS)
POOL_WIDTH = D_MODEL // 4
POOL_GROUP_DIM = POOL_WIDTH // POOL_GROUPS

LRU_WIDTH = D_MODEL // 4
LRU_BLOCKS = 4
LRU_BLOCK_DIM = LRU_WIDTH // LRU_BLOCKS
LRU_C = 8.0
CONV_WIDTH = 4

MIX_WIDTH = ATTN_WIDTH + POOL_WIDTH + LRU_WIDTH
IN_WIDTH = 3 * ATTN_WIDTH + POOL_WIDTH + 2 * LRU_WIDTH
IN_SPLITS = (ATTN_WIDTH, 2 * ATTN_WIDTH, 3 * ATTN_WIDTH,
             3 * ATTN_WIDTH + POOL_WIDTH, 3 * ATTN_WIDTH + POOL_WIDTH + LRU_WIDTH)

D_FF = ((8 * D_MODEL // 3 + 127) // 128) * 128
N_EXPERTS = 8
TOP_K = 2
MOE_BLOCK = 256

LN_EPS = 1e-5
HEAD_NORM_EPS = 1e-5

kernel_name = 'hybrid_diffattn_pool_rglru_moe_deepnorm'


def _layer_norm(x, g, b):
    xf = x.astype(jnp.float32)
    mu = jnp.mean(xf, axis=-1, keepdims=True)
    var = jnp.mean(jnp.square(xf - mu), axis=-1, keepdims=True)
    return ((xf - mu) * lax.rsqrt(var + LN_EPS) * g + b).astype(x.dtype)


def _diff_attention(q, k, v, lam, subln_g, lambda_init):
    B, S = q.shape[0], q.shape[1]
    scale = ATTN_HEAD_DIM ** -0.5
    outs = []
    for blk in range(S // Q_BLOCK):
        q0 = blk * Q_BLOCK
        k_end = q0 + Q_BLOCK
        qb = q[:, q0:k_end]
        kb = k[:, :k_end]
        vb = v[:, :k_end]
        s = jnp.einsum('bqhmd,bkhmd->bhmqk', qb, kb).astype(jnp.float32) * scale
        q_chunk = (q0 + jnp.arange(Q_BLOCK)) // CHUNK
        k_chunk = jnp.arange(k_end) // CHUNK
        mask = k_chunk[None, :] <= q_chunk[:, None]
        s = jnp.where(mask, s, -jnp.inf)
        p = jax.nn.softmax(s, axis=-1)
        p_diff = p[:, :, 0] - lam * p[:, :, 1]
        outs.append(jnp.einsum('bhqk,bkhe->bqhe', p_diff.astype(v.dtype), vb))
    o = jnp.concatenate(outs, axis=1).astype(jnp.float32)
    o = o * lax.rsqrt(jnp.mean(jnp.square(o), axis=-1, keepdims=True) + HEAD_NORM_EPS)
    o = o * subln_g * (1.0 - lambda_init)
    return o.reshape(B, S, ATTN_WIDTH).astype(v.dtype)


def _pool_mixer(u, pool_w, pool_scale):
    B, S, _ = u.shape
    ug = u.reshape(B, S, POOL_GROUPS, POOL_GROUP_DIM)
    c = jnp.cumsum(ug.astype(jnp.float32), axis=1)
    c = jnp.pad(c, ((0, 0), (1, 0), (0, 0), (0, 0)))
    t = jnp.arange(S)
    win = jnp.array(POOL_WINDOWS, dtype=jnp.int32)
    lo = jnp.maximum(t[:, None] + 1 - win[None, :], 0)
    grp = jnp.arange(POOL_GROUPS)
    window_sum = c[:, 1:] - c[:, lo, grp]
    count = (t[:, None] + 1 - lo).astype(jnp.float32)
    pooled = window_sum / count[None, :, :, None] - ug.astype(jnp.float32)
    y = jnp.einsum('bsgc,gcd->bsgd', pooled.astype(u.dtype), pool_w)
    return y.reshape(B, S, POOL_WIDTH) * pool_scale


def _linear_combine(left, right):
    a_l, b_l = left
    a_r, b_r = right
    return a_l * a_r, a_r * b_l + b_r


def _rglru_mixer(xr, xg, conv_w, conv_b, w_a, b_a, w_i, b_i, lam):
    B, S, W = xr.shape
    xp = jnp.pad(xr, ((0, 0), (CONV_WIDTH - 1, 0), (0, 0)))
    xc = conv_b + xp[:, 0:S] * conv_w[0]
    for j in range(1, CONV_WIDTH):
        xc = xc + xp[:, j:j + S] * conv_w[j]
    xb = xc.reshape(B, S, LRU_BLOCKS, LRU_BLOCK_DIM)
    r = jax.nn.sigmoid((jnp.einsum('bshc,hcd->bshd', xb, w_a).reshape(B, S, W) + b_a).astype(jnp.float32))
    i = jax.nn.sigmoid((jnp.einsum('bshc,hcd->bshd', xb, w_i).reshape(B, S, W) + b_i).astype(jnp.float32))
    log_a = -LRU_C * r * jax.nn.softplus(-lam.astype(jnp.float32))
    a = jnp.exp(log_a)
    bterm = jnp.sqrt(-jnp.expm1(2.0 * log_a)) * (i * xc.astype(jnp.float32))
    _, h = lax.associative_scan(_linear_combine, (a, bterm), axis=1)
    return h.astype(xr.dtype) * jax.nn.gelu(xg)


def _swiglu(x, w_gate, w_up, w_down):
    return (jax.nn.silu(x @ w_gate) * (x @ w_up)) @ w_down


def _moe_swiglu(x, w_router, w_gate, w_up, w_down):
    B, S, D = x.shape
    xt = x.reshape(-1, D)
    T = xt.shape[0]
    P = T * TOP_K
    logits = (xt @ w_router).astype(jnp.float32)
    top_logit, top_e = lax.top_k(logits, TOP_K)
    gate = jax.nn.softmax(top_logit, axis=-1)
    flat_e = top_e.reshape(-1)
    order = jnp.argsort(flat_e)
    e_sorted = flat_e[order]
    tok = order // TOP_K
    sizes = jnp.bincount(flat_e, length=N_EXPERTS)
    padded = (sizes + MOE_BLOCK - 1) // MOE_BLOCK * MOE_BLOCK
    start = jnp.cumsum(sizes) - sizes
    pstart = jnp.cumsum(padded) - padded
    dest = pstart[e_sorted] + jnp.arange(P) - start[e_sorted]
    n_blocks = -(-P // MOE_BLOCK) + N_EXPERTS
    rows = jnp.zeros((n_blocks * MOE_BLOCK, D), x.dtype).at[dest].set(xt[tok])
    block_start = jnp.arange(n_blocks) * MOE_BLOCK
    block_e = jnp.minimum(jnp.sum(jnp.cumsum(padded)[None, :] <= block_start[:, None], axis=1), N_EXPERTS - 1)

    def expert_block(args):
        xb, e = args
        return _swiglu(xb, w_gate[e], w_up[e], w_down[e])

    ys = lax.map(expert_block, (rows.reshape(n_blocks, MOE_BLOCK, D), block_e)).reshape(-1, D)
    contrib = ys[dest] * gate.reshape(-1)[order][:, None].astype(x.dtype)
    return jnp.zeros_like(xt).at[tok].add(contrib).reshape(B, S, D)


def setup_inputs(seed: int = 0) -> dict:
    key = jax.random.key(seed)
    ks = iter(jax.random.split(key, 32))
    n_dense = (DEPTH + 1) // 2
    n_moe = DEPTH // 2
    beta = (8.0 * DEPTH) ** -0.25
    f32 = jnp.float32

    def nrm(shape, fan_in, scale=1.0):
        return jax.random.normal(next(ks), shape, f32) * (scale * fan_in ** -0.5)

    def noise(shape, s):
        return jax.random.normal(next(ks), shape, f32) * s

    x = jax.random.normal(next(ks), (BATCH, SEQ, D_MODEL), f32)
    col_scale = jnp.concatenate([
        jnp.ones((2 * ATTN_WIDTH,), f32),
        jnp.full((ATTN_WIDTH + POOL_WIDTH + LRU_WIDTH,), beta, f32),
        jnp.ones((LRU_WIDTH,), f32)])
    w_in = nrm((DEPTH, D_MODEL, IN_WIDTH), D_MODEL) * col_scale
    w_out = nrm((DEPTH, MIX_WIDTH, D_MODEL), MIX_WIDTH, beta)
    attn_lambda = noise((DEPTH, 4, ATTN_HEAD_DIM), 0.1)
    attn_subln_g = 1.0 + noise((DEPTH, 2 * ATTN_HEAD_DIM), 0.02)
    pool_w = nrm((DEPTH, POOL_GROUPS, POOL_GROUP_DIM, POOL_GROUP_DIM), POOL_GROUP_DIM)
    pool_scale = 1.0 + noise((DEPTH, POOL_WIDTH), 0.02)
    conv_w = nrm((DEPTH, CONV_WIDTH, LRU_WIDTH), CONV_WIDTH)
    conv_b = noise((DEPTH, LRU_WIDTH), 0.02)
    lru_wa = nrm((DEPTH, LRU_BLOCKS, LRU_BLOCK_DIM, LRU_BLOCK_DIM), LRU_BLOCK_DIM)
    lru_ba = noise((DEPTH, LRU_WIDTH), 0.02)
    lru_wi = nrm((DEPTH, LRU_BLOCKS, LRU_BLOCK_DIM, LRU_BLOCK_DIM), LRU_BLOCK_DIM)
    lru_bi = noise((DEPTH, LRU_WIDTH), 0.02)
    a_c = jax.random.uniform(next(ks), (DEPTH, LRU_WIDTH), f32, 0.9, 0.999)
    a0 = a_c ** (1.0 / LRU_C)
    lru_lambda = jnp.log(a0) - jnp.log1p(-a0)
    ln1_g = 1.0 + noise((DEPTH, D_MODEL), 0.02)
    ln1_b = noise((DEPTH, D_MODEL), 0.02)
    ln2_g = 1.0 + noise((DEPTH, D_MODEL), 0.02)
    ln2_b = noise((DEPTH, D_MODEL), 0.02)
    ffn_w_gate = nrm((n_dense, D_MODEL, D_FF), D_MODEL, beta)
    ffn_w_up = nrm((n_dense, D_MODEL, D_FF), D_MODEL, beta)
    ffn_w_down = nrm((n_dense, D_FF, D_MODEL), D_FF, beta)
    router_w = nrm((n_moe, D_MODEL, N_EXPERTS), D_MODEL)
    moe_w_gate = nrm((n_moe, N_EXPERTS, D_MODEL, D_FF), D_MODEL, beta)
    moe_w_up = nrm((n_moe, N_EXPERTS, D_MODEL, D_FF), D_MODEL, beta)
    moe_w_down = nrm((n_moe, N_EXPERTS, D_FF, D_MODEL), D_FF, beta)
    return {'x': x, 'w_in': w_in, 'w_out': w_out, 'attn_lambda': attn_lambda,
            'attn_subln_g': attn_subln_g, 'pool_w': pool_w, 'pool_scale': pool_scale,
            'conv_w': conv_w, 'conv_b': conv_b, 'lru_wa': lru_wa, 'lru_ba': lru_ba,
            'lru_wi': lru_wi, 'lru_bi': lru_bi, 'lru_lambda': lru_lambda,
            'ln1_g': ln1_g, 'ln1_b': ln1_b, 'ln2_g': ln2_g, 'ln2_b': ln2_b,
            'ffn_w_gate': ffn_w_gate, 'ffn_w_up': ffn_w_up, 'ffn_w_down': ffn_w_down,
            'router_w': router_w, 'moe_w_gate': moe_w_gate, 'moe_w_up': moe_w_up,
            'moe_w_down': moe_w_down}


def reference(x, w_in, w_out, attn_lambda, attn_subln_g, pool_w, pool_scale,
              conv_w, conv_b, lru_wa, lru_ba, lru_wi, lru_bi, lru_lambda,
              ln1_g, ln1_b, ln2_g, ln2_b, ffn_w_gate, ffn_w_up, ffn_w_down,
              router_w, moe_w_gate, moe_w_up, moe_w_down):
    alpha = (2.0 * DEPTH) ** 0.25
    B, S, _ = x.shape
    for l in range(DEPTH):
        h = x @ w_in[l]
        q, k, v, u, xr, xg = jnp.split(h, IN_SPLITS, axis=-1)
        q = q.reshape(B, S, ATTN_HEADS, 2, ATTN_HEAD_DIM)
        k = k.reshape(B, S, ATTN_HEADS, 2, ATTN_HEAD_DIM)
        v = v.reshape(B, S, ATTN_HEADS, 2 * ATTN_HEAD_DIM)
        lambda_init = 0.8 - 0.6 * math.exp(-0.3 * l)
        lv = attn_lambda[l].astype(jnp.float32)
        lam = jnp.exp(jnp.sum(lv[0] * lv[1])) - jnp.exp(jnp.sum(lv[2] * lv[3])) + lambda_init
        y_attn = _diff_attention(q, k, v, lam, attn_subln_g[l], lambda_init)
        y_pool = _pool_mixer(u, pool_w[l], pool_scale[l])
        y_lru = _rglru_mixer(xr, xg, conv_w[l], conv_b[l], lru_wa[l], lru_ba[l],
                             lru_wi[l], lru_bi[l], lru_lambda[l])
        mix = jnp.concatenate([y_attn, y_pool, y_lru], axis=-1) @ w_out[l]
        x = _layer_norm(alpha * x + mix, ln1_g[l], ln1_b[l])
        if l % 2 == 0:
            f = _swiglu(x, ffn_w_gate[l // 2], ffn_w_up[l // 2], ffn_w_down[l // 2])
        else:
            f = _moe_swiglu(x, router_w[l // 2], moe_w_gate[l // 2], moe_w_up[l // 2], moe_w_down[l // 2])
        x = _layer_norm(alpha * x + f, ln2_g[l], ln2_b[l])
    return x
```

```python
import math
from contextlib import ExitStack
import numpy as np
import concourse.bass as bass
import concourse.mybir as mybir
from concourse.bass_utils import run_bass_kernel_spmd

F32 = mybir.dt.float32
BF16 = mybir.dt.bfloat16
AF = mybir.ActivationFunctionType
ALU = mybir.AluOpType
AX = mybir.AxisListType

D = 1024
S_LEN = 2048
DEPTH = 4
NT = 16
DFF = 2816
NE = 8
ALPHA = (2.0 * DEPTH) ** 0.25
LN_EPS = 1e-5
HN_EPS = 1e-5
POOL_W = (2, 4, 8, 16)
GROUPS = [(0, 4), (4, 4), (8, 4), (12, 4), (16, 4), (20, 2)]


class Res:
    __slots__ = ("name", "w", "rs", "rd")

    def __init__(self, name):
        self.name = name
        self.w = None
        self.rs = {}
        self.rd = []


class Op:
    __slots__ = ("eng", "fn", "deps", "signal", "done", "is_dma", "sem")


class _Rec:
    def __init__(self):
        self.call = None

    def __getattr__(self, name):
        def f(*a, **k):
            self.call = (name, a, k)
            return self
        return f


class Sched:
    ENGS = ("pe", "act", "dve", "pool", "sp")

    def __init__(self):
        self.ops = {e: [] for e in self.ENGS}
        self.dmas_since_barrier = []
        self.final = []

    def op(self, eng, fn, r=(), w=(), dma_sem=None, extra_deps=()):
        o = Op()
        rec = _Rec()
        fn(rec)
        o.eng, o.fn, o.deps, o.signal, o.done = eng, rec.call, [], False, None
        o.is_dma, o.sem = dma_sem is not None, dma_sem
        deps = {}
        for res in r:
            if res.w is not None:
                deps[res.w] = True
        for res in w:
            if res.w is not None and res.w not in deps:
                deps[res.w] = False
            for rd in res.rs.values():
                if rd not in deps:
                    deps[rd] = False
            for rd in res.rd:
                if rd not in deps:
                    deps[rd] = False
        for d in extra_deps:
            deps[d] = True
        for d, raw in deps.items():
            if d is o:
                continue
            if (not d.is_dma) and d.eng == eng and not raw and eng == "pe":
                continue
            o.deps.append(d)
            d.signal = True
        for res in r:
            if o.is_dma:
                res.rd.append(o)
            else:
                res.rs[eng] = o
        for res in w:
            res.w = o
            res.rs = {}
            res.rd = []
        if o.is_dma:
            o.signal = True
            self.dmas_since_barrier.append(o)
        self.ops[eng].append(o)
        return o

    def barrier(self):
        lasts = []
        for e in self.ENGS:
            for o in reversed(self.ops[e]):
                if not o.is_dma and o.fn is not None:
                    lasts.append(o)
                    break
        deps = lasts + self.dmas_since_barrier
        self.dmas_since_barrier = []
        for e in self.ENGS:
            o = Op()
            o.eng, o.fn, o.deps, o.signal, o.done, o.is_dma, o.sem = e, None, [], False, None, False, None
            for d in deps:
                if d.eng == e and not d.is_dma and e == "pe":
                    continue
                o.deps.append(d)
                d.signal = True
            self.ops[e].append(o)

    def emit(self, nc):
        dma_counts = {}
        sem_names = set()
        for e in self.ENGS:
            cnt = 0
            for o in self.ops[e]:
                if o.is_dma:
                    c = dma_counts.get(o.sem, 0) + 16
                    dma_counts[o.sem] = c
                    o.done = ("d_" + o.sem, c)
                    sem_names.add(o.done[0])
                elif o.signal:
                    cnt += 1
                    o.done = ("e_" + e, cnt)
                    sem_names.add(o.done[0])
        engobj = {"pe": nc.tensor, "act": nc.scalar, "dve": nc.vector, "pool": nc.gpsimd, "sp": nc.sync}
        with ExitStack() as st:
            sems = {n: st.enter_context(nc.semaphore(n)) for n in sorted(sem_names)}
            block = st.enter_context(nc.Block())

            def run(ename):
                def body(e):
                    waited = {}
                    for o in self.ops[ename]:
                        need = {}
                        for d in o.deps:
                            sname, val = d.done
                            if need.get(sname, 0) < val:
                                need[sname] = val
                        for sname, val in need.items():
                            if waited.get(sname, 0) >= val:
                                continue
                            e.wait_ge(sems[sname], val)
                            waited[sname] = val
                        if o.fn is None:
                            continue
                        name_, a_, k_ = o.fn
                        ins = getattr(e, name_)(*a_, **k_)
                        if o.signal:
                            ins.then_inc(sems[o.done[0]], 16 if o.is_dma else 1)
                    if ename == "sp":
                        for o in self.final:
                            e.wait_ge(sems[o.done[0]], o.done[1])
                return body

            block.tensor(run("pe"))
            block.scalar(run("act"))
            block.vector(run("dve"))
            block.gpsimd(run("pool"))
            block.sync(run("sp"))


_DBG = {}


def build_program(n_layers=DEPTH):
    nc = bass.Bass("TRN2", target_bir_lowering=False)
    S = Sched()

    def din(name, shape):
        return nc.dram_tensor(name, list(shape), F32, kind="ExternalInput").ap()

    x_d = din("x", [S_LEN, D])
    w_in_d = din("w_in", [DEPTH, D, 2304])
    w_out_d = din("w_out", [DEPTH, D, D])
    lam_d = din("attn_lambda", [DEPTH * 4 * 64])
    subg_d = din("attn_subln_g", [DEPTH * 128])
    bdw_d = din("bdw", [DEPTH, 6, 128, 128])
    pp_d = din("pp", [128, DEPTH, 18])
    ln_d = din("lnp", [DEPTH, 4, D])
    fg_d = din("ffn_w_gate", [2, D, DFF])
    fu_d = din("ffn_w_up", [2, D, DFF])
    fd_d = din("ffn_w_down", [2, DFF, D])
    rt_d = din("router_w", [2, D, NE])
    mg_d = din("moe_w_gate", [2, NE, D, DFF])
    mu_d = din("moe_w_up", [2, NE, D, DFF])
    md_d = din("moe_w_down", [2, NE, DFF, D])
    ident_d = din("ident", [128, 128])
    out_d = nc.dram_tensor("out", [S_LEN, D], F32, kind="ExternalOutput").ap()

    cur = [16384]

    def alloc(name, shape, dt, at=None):
        nbytes = int(np.prod(shape[1:])) * (4 if dt == F32 else 2)
        nbytes = (nbytes + 63) // 64 * 64
        if at is None:
            off = cur[0]
            cur[0] += nbytes
        else:
            off = at
        assert off + nbytes <= 16384 + 212800, (name, off, nbytes)
        return nc.alloc_sbuf_tensor_at(name, list(shape), dt, offset=off), off + nbytes

    XRES, _ = alloc("xres", [128, NT, D], F32)
    XT, _ = alloc("xt", [128, 8, S_LEN], BF16)
    LNBC, _ = alloc("lnbc", [128, 4, D], F32)
    IDENT, _ = alloc("ident_sb", [128, 128], F32)
    PP, _ = alloc("pp_sb", [128, DEPTH, 18], F32)
    NSP, _ = alloc("nsp", [128, DEPTH, 2, 2], F32)
    LAMV, _ = alloc("lamv", [128, 16], F32)
    SUBG, _ = alloc("subg", [128, DEPTH, 128], F32)
    INVW, _ = alloc("invw", [128, 2], F32)
    INVC, _ = alloc("invc", [128, 2, 16], F32)
    BDW, _ = alloc("bdw_sb", [128, 2, 6, 128], BF16)
    GW, _ = alloc("gw", [128, NT, NE], F32)
    SCR, scr_end = alloc("scr", [128, 1024], F32)
    scr0 = scr_end - 4096
    MASKB, _ = alloc("maskb", [128, 2, 256], BF16, at=scr0)
    MASKA, _ = alloc("maska", [128, 128], BF16, at=scr0 + 864 * 4)
    WORK0 = cur[0]

    def walloc(name, shape, dt, off):
        t, end = alloc(name, shape, dt, at=WORK0 + off)
        return t, end - WORK0

    YT, o1 = walloc("yt", [128, 8, S_LEN], BF16, 0)
    FB = []
    o = o1
    for i in range(5):
        t, o = walloc("fb%d" % i, [128, S_LEN], F32, o)
        FB.append(t)
    HB, o = walloc("hb", [128, S_LEN], BF16, o)
    WA, o = walloc("wa", [128, 8, 384], BF16, o)
    LAMB, _ = walloc("lamb", [128, DEPTH * 256], F32, o1)
    V1, o = walloc("v1", [128, NT, 4, 130], BF16, o1)
    QT0, o = walloc("qt0", [128, S_LEN], BF16, o)
    QT1, o = walloc("qt1", [128, S_LEN], BF16, o)
    KT, o = walloc("kt", [128, S_LEN], BF16, o)
    PT, o = walloc("pt", [128, 2, 16, 256], BF16, o)
    WQK, o = walloc("wqk", [128, 1, 8, 256], BF16, o)
    OST, o = walloc("ost", [128, 2, 4, 128], F32, o)
    WV, _ = walloc("wv", [128, 8, 512], BF16, o1 + 16640 + 3 * 4096)
    WO, _ = walloc("wo", [128, 8, D], BF16, o1)
    ACTT, o = walloc("actt", [128, 2, 4, S_LEN], BF16, 0)
    WG, o = walloc("wg", [128, 2, 8, 512], BF16, o)
    WU, o = walloc("wu", [128, 2, 8, 512], BF16, o)
    WD, o = walloc("wd", [128, 2, 4, D], BF16, o)
    SIL, o = walloc("sil", [128, 2, 512], BF16, o)
    WR, o = walloc("wr", [128, 8, NE], F32, o)
    XLO, o = walloc("xlo", [128, 8, 128], BF16, o)
    WRH, o = walloc("wrh", [128, 8, NE], BF16, o)
    WRL, o = walloc("wrl", [128, 8, NE], BF16, o)

    PSALL = nc.alloc_psum_tensor("psall", [128, 4096], F32)
    PS = [PSALL[:, i * 512:(i + 1) * 512] for i in range(8)]
    PSR = [Res("ps%d" % i) for i in range(8)]

    R_xres = [Res("xres%d" % i) for i in range(NT)]
    R_xt = [Res("xt%d" % i) for i in range(NT)]
    R_yt = [[Res("yt%d_%d" % (c, i)) for i in range(4)] for c in range(8)]
    R_fb = [[Res("fb%d_%d" % (b, i)) for i in range(4)] for b in range(5)]
    R_hb = [Res("hb%d" % i) for i in range(4)]
    R_misc = {}

    def RM(name):
        if name not in R_misc:
            R_misc[name] = Res(name)
        return R_misc[name]

    def xt_blk(n0, n1):
        return R_xt[n0:n1]

    ring_ctr = {}

    def ring(name, n):
        c = ring_ctr.get(name, 0)
        ring_ctr[name] = c + 1
        return c % n

    S.op("sp", lambda e: e.dma_start(out=IDENT[:], in_=ident_d), w=[RM("ident")], dma_sem="ident")
    S.op("sp", lambda e: e.dma_start(out=PP[:], in_=pp_d), w=[RM("pp")], dma_sem="pp")
    for q in range(4):
        S.op("sp", lambda e, q=q: e.dma_start(
            out=XRES[:, 4 * q:4 * q + 4, :],
            in_=x_d[512 * q:512 * (q + 1), :].rearrange("(t p) d -> p t d", p=128)),
            w=R_xres[4 * q:4 * q + 4], dma_sem="xin%d" % q)
    S.op("sp", lambda e: e.dma_start(out=LAMB[:], in_=lam_d.partition_broadcast(128)), w=[RM("lamb")], dma_sem="lamb")
    S.op("sp", lambda e: e.dma_start(out=SUBG[:].rearrange("p l e -> p (l e)"), in_=subg_d.partition_broadcast(128)),
         w=[RM("subg")], dma_sem="subg")

    lamb4 = LAMB[:].rearrange("p (l j e) -> p l j e", l=DEPTH, j=4)
    prod = SCR[:, 0:512].rearrange("p (l j e) -> p l j e", l=DEPTH, j=2)
    S.op("dve", lambda e: e.tensor_tensor(out=prod[:, :, 0, :], in0=lamb4[:, :, 0, :], in1=lamb4[:, :, 1, :], op=ALU.mult),
         r=[RM("lamb")], w=[RM("scr_a")])
    S.op("dve", lambda e: e.tensor_tensor(out=prod[:, :, 1, :], in0=lamb4[:, :, 2, :], in1=lamb4[:, :, 3, :], op=ALU.mult),
         r=[RM("lamb")], w=[RM("scr_b")])
    S.op("dve", lambda e: e.reduce_sum(out=SCR[:, 512:520], in_=SCR[:, 0:512].rearrange("p (k e) -> p k e", e=64), axis=AX.X),
         r=[RM("scr_a"), RM("scr_b")], w=[RM("scr_c")])
    S.op("act", lambda e: e.activation(out=SCR[:, 520:528], in_=SCR[:, 512:520], func=AF.Exp), r=[RM("scr_c")], w=[RM("scr_d")])
    ex8 = SCR[:, 520:528].rearrange("p (l j) -> p l j", j=2)
    S.op("dve", lambda e: e.tensor_tensor(out=LAMV[:, 0:DEPTH], in0=ex8[:, :, 0], in1=ex8[:, :, 1], op=ALU.subtract),
         r=[RM("scr_d")], w=[RM("lamv")])
    for l in range(DEPTH):
        li = 0.8 - 0.6 * math.exp(-0.3 * l)
        S.op("dve", lambda e, l=l, li=li: e.tensor_scalar(out=LAMV[:, l:l + 1], in0=LAMV[:, l:l + 1], scalar1=li, scalar2=None,
                                                           op0=ALU.add), r=[RM("lamv")], w=[RM("lamv")])
        S.op("dve", lambda e, l=l, li=li: e.tensor_scalar(out=SUBG[:, l, :], in0=SUBG[:, l, :], scalar1=1.0 - li, scalar2=None,
                                                           op0=ALU.mult), r=[RM("subg")], w=[RM("subg")])
    for c in range(2):
        lamcol = PP[:, :, c * 9 + 7]
        S.op("act", lambda e, lamcol=lamcol, c=c: e.activation(out=SCR[:, 528 + 4 * c:532 + 4 * c], in_=lamcol, func=AF.Exp, scale=-1.0),
             r=[RM("pp")], w=[RM("scr_e%d" % c)])
        S.op("act", lambda e, c=c: e.activation(out=SCR[:, 536 + 4 * c:540 + 4 * c], in_=SCR[:, 528 + 4 * c:532 + 4 * c], func=AF.Ln, bias=1.0),
             r=[RM("scr_e%d" % c)], w=[RM("scr_f%d" % c)])
        S.op("dve", lambda e, c=c: e.tensor_scalar(out=NSP[:, :, c, 0], in0=SCR[:, 536 + 4 * c:540 + 4 * c], scalar1=-8.0, scalar2=None, op0=ALU.mult),
             r=[RM("scr_f%d" % c)], w=[RM("nsp")])
        S.op("dve", lambda e, c=c: e.tensor_scalar(out=NSP[:, :, c, 1], in0=SCR[:, 536 + 4 * c:540 + 4 * c], scalar1=-16.0, scalar2=None, op0=ALU.mult),
             r=[RM("scr_f%d" % c)], w=[RM("nsp")])
    for c in range(2):
        for hf in range(2):
            wdw = POOL_W[2 * c + hf]
            ps_ = slice(64 * hf, 64 * hf + 64)
            S.op("dve", lambda e, c=c, ps_=ps_, wdw=wdw: e.memset(INVW[ps_, c:c + 1], 1.0 / wdw), w=[RM("invw")])
            S.op("dve", lambda e, c=c, ps_=ps_, wdw=wdw: e.memset(INVC[ps_, c, :], 1.0 / wdw), w=[RM("invc")])
            for t in range(min(wdw - 1, 16)):
                S.op("dve", lambda e, c=c, ps_=ps_, t=t: e.memset(INVC[ps_, c, t:t + 1], 1.0 / (t + 1)), w=[RM("invc")])
    S.barrier()

    def emit_transposes(tt, router_bank=None):
        for g4 in range(2):
            b = 4 + ring("tr", 4)
            for kk in range(4):
                k = 4 * g4 + kk
                S.op("pe", lambda e, b=b, kk=kk, k=k: e.transpose(out=PS[b][:, kk * 128:(kk + 1) * 128],
                                                                in_=XRES[:, tt, k * 128:(k + 1) * 128], identity=IDENT[:]),
                     r=[R_xres[tt], RM("ident")], w=[PSR[b]])
            S.op("act", lambda e, b=b, g4=g4: e.activation(
                out=XT[:, 4 * g4:4 * g4 + 4, tt * 128:(tt + 1) * 128],
                in_=PS[b][:].rearrange("p (k t) -> p k t", k=4), func=AF.Identity, scale=1.0 / ALPHA),
                r=[PSR[b]], w=[R_xt[tt]])
            if router_bank is not None:
                S.op("dve", lambda e, b=b, g4=g4: e.scalar_tensor_tensor(
                    out=XLO[:, 4 * g4:4 * g4 + 4, :], in0=PS[b][:].rearrange("p (k t) -> p k t", k=4), scalar=1.0 / ALPHA,
                    in1=XT[:, 4 * g4:4 * g4 + 4, tt * 128:(tt + 1) * 128], op0=ALU.mult, op1=ALU.subtract),
                    r=[PSR[b], R_xt[tt]], w=[RM("xlo%d" % g4)])
        if router_bank is not None:
            n = 0
            for k in range(8):
                for (lh, lres, rh) in ((XT[:, k, tt * 128:(tt + 1) * 128], R_xt[tt], WRH), (XLO[:, k, :], RM("xlo%d" % (k // 4)), WRH),
                                       (XT[:, k, tt * 128:(tt + 1) * 128], R_xt[tt], WRL)):
                    S.op("pe", lambda e, lh=lh, rh=rh, k=k, n=n: e.matmul(PS[router_bank][:, 0:NE], lhsT=lh, rhs=rh[:, k, :],
                                                                        start=(n == 0), stop=(n == 23)),
                         r=[RM("wrhl"), lres], w=[PSR[router_bank]])
                    n += 1

    for tt in range(NT):
        S.op("act", lambda e, tt=tt: e.activation(out=XRES[:, tt, :], in_=XRES[:, tt, :], func=AF.Identity, scale=ALPHA),
             r=[R_xres[tt]], w=[R_xres[tt]])
        emit_transposes(tt)

    for l in range(n_layers):
        is_moe = (l % 2 == 1) and not _DBG.get('force_dense')
        last = (l == n_layers - 1)
        S.barrier()
        bs = l % 2
        S.op("pool", lambda e, l=l, bs=bs: e.dma_start(out=BDW[:, bs, :, :], in_=bdw_d[l].rearrange("s k m -> k s m")),
             w=[RM("bdw%d" % bs)], dma_sem="bdw%d" % bs)
        S.op("sp", lambda e, l=l: e.dma_start(out=LNBC[:].rearrange("p a d -> p (a d)"),
                                              in_=ln_d[l].rearrange("a d -> (a d)").partition_broadcast(128)),
             w=[RM("lnbc")], dma_sem="lnbc")
        for a in range(4):
            if last and a >= 2:
                continue
            S.op("act", lambda e, a=a: e.activation(out=LNBC[:, a, :], in_=LNBC[:, a, :], func=AF.Identity, scale=ALPHA),
                 r=[RM("lnbc")], w=[RM("lnbc")])

        for c in range(2):
            ppc = lambda j, l=l, c=c: PP[:, l, c * 9 + j:c * 9 + j + 1]
            for si, col0 in enumerate((1536 + 128 * c, 1792 + 128 * c, 2048 + 128 * c)):
                S.op("pool", lambda e, l=l, si=si, col0=col0: e.dma_start(
                    out=WA[:, :, si * 128:(si + 1) * 128],
                    in_=w_in_d[l][:, col0:col0 + 128].rearrange("(k p) n -> p k n", p=128)),
                    w=[RM("wa%d" % si)], dma_sem="wa%d" % si)
            U, XR, XG, A_, B_ = FB[0], FB[1], FB[2], FB[3], FB[4]
            for si, dst, dr in ((0, U, 0), (1, XR, 1), (2, XG, 2)):
                for nt in range(4):
                    b = ring("pa", 4)
                    for k in range(8):
                        S.op("pe", lambda e, b=b, si=si, k=k, nt=nt: e.matmul(
                            PS[b][:], lhsT=WA[:, k, si * 128:(si + 1) * 128], rhs=XT[:, k, nt * 512:(nt + 1) * 512],
                            start=(k == 0), stop=(k == 7)),
                            r=[RM("wa%d" % si)] + xt_blk(4 * nt, 4 * nt + 4), w=[PSR[b]])
                    S.op("act", lambda e, b=b, dst=dst, nt=nt: e.activation(out=dst[:, nt * 512:(nt + 1) * 512], in_=PS[b][:], func=AF.Copy),
                         r=[PSR[b]], w=[R_fb[dr][nt]])
            def shift_add(dst, di, src, si_, k):
                S.op("dve", lambda e: e.tensor_tensor(out=dst[:, k:], in0=src[:, k:], in1=src[:, :S_LEN - k], op=ALU.add),
                     r=R_fb[si_], w=R_fb[di])
                S.op("act", lambda e: e.activation(out=dst[:, 0:k], in_=src[:, 0:k], func=AF.Copy), r=R_fb[si_], w=R_fb[di])
            shift_add(A_, 3, U, 0, 1)
            shift_add(B_, 4, A_, 3, 2)
            if c == 1:
                shift_add(A_, 3, B_, 4, 4)
                shift_add(B_, 4, A_, 3, 8)
            for hf, ws, wr in ((0, A_, 3), (1, B_, 4)):
                ps_ = slice(64 * hf, 64 * hf + 64)
                S.op("dve", lambda e, ps_=ps_, ws=ws: e.scalar_tensor_tensor(
                    out=HB[ps_, 16:], in0=ws[ps_, 16:], scalar=INVW[ps_, c:c + 1], in1=U[ps_, 16:], op0=ALU.mult, op1=ALU.subtract),
                    r=R_fb[wr] + R_fb[0] + [RM("invw")], w=R_hb)
                S.op("dve", lambda e, ps_=ps_, ws=ws, hf=hf: e.tensor_tensor(out=SCR[ps_, 600 + 16 * hf:616 + 16 * hf], in0=ws[ps_, 0:16], in1=INVC[ps_, c, :], op=ALU.mult),
                     r=R_fb[wr] + [RM("invc")], w=[RM("scr_p%d" % hf)])
                S.op("dve", lambda e, ps_=ps_, hf=hf: e.tensor_tensor(out=HB[ps_, 0:16], in0=SCR[ps_, 600 + 16 * hf:616 + 16 * hf], in1=U[ps_, 0:16], op=ALU.subtract),
                     r=R_fb[0] + [RM("scr_p%d" % hf)], w=R_hb)
            for nt in range(4):
                b = 4 + ring("pb", 4)
                S.op("pe", lambda e, b=b, nt=nt: e.matmul(PS[b][:], lhsT=BDW[:, bs, c, :], rhs=HB[:, nt * 512:(nt + 1) * 512], start=True, stop=True),
                     r=[RM("bdw%d" % bs)] + R_hb, w=[PSR[b]])
                S.op("act", lambda e, b=b, nt=nt: e.activation(out=YT[:, 4 + c, nt * 512:(nt + 1) * 512], in_=PS[b][:], func=AF.Identity, scale=ppc(8)),
                     r=[PSR[b], RM("pp")], w=[R_yt[4 + c][nt]])
            XC = A_
            S.op("dve", lambda e: e.tensor_scalar(out=XC[:], in0=XR[:], scalar1=ppc(3), scalar2=ppc(4), op0=ALU.mult, op1=ALU.add),
                 r=R_fb[1] + [RM("pp")], w=R_fb[3])
            for j, sh in ((2, 1), (1, 2), (0, 3)):
                S.op("dve", lambda e, j=j, sh=sh: e.scalar_tensor_tensor(
                    out=XC[:, sh:], in0=XR[:, :S_LEN - sh], scalar=ppc(j), in1=XC[:, sh:], op0=ALU.mult, op1=ALU.add),
                    r=R_fb[1] + R_fb[3] + [RM("pp")], w=R_fb[3])
            S.op("act", lambda e: e.activation(out=HB[:], in_=XC[:], func=AF.Copy), r=R_fb[3], w=R_hb)
            Rg, Ig = XR, B_
            for (wi_, dst, dr, bcol) in ((2 + c, Rg, 1, 5), (4 + c, Ig, 4, 6)):
                for nt in range(4):
                    b = 4 + ring("pb", 4)
                    S.op("pe", lambda e, b=b, nt=nt, wi_=wi_: e.matmul(PS[b][:], lhsT=BDW[:, bs, wi_, :], rhs=HB[:, nt * 512:(nt + 1) * 512], start=True, stop=True),
                         r=[RM("bdw%d" % bs)] + R_hb, w=[PSR[b]])
                    S.op("act", lambda e, b=b, nt=nt, dst=dst, bcol=bcol: e.activation(out=dst[:, nt * 512:(nt + 1) * 512], in_=PS[b][:], func=AF.Sigmoid, bias=ppc(bcol)),
                         r=[PSR[b], RM("pp")], w=[R_fb[dr][nt]])
            S.op("dve", lambda e: e.tensor_tensor(out=Ig[:], in0=Ig[:], in1=XC[:], op=ALU.mult), r=R_fb[4] + R_fb[3], w=R_fb[4])
            A2 = XC
            S.op("act", lambda e: e.activation(out=A2[:], in_=Rg[:], func=AF.Exp, scale=NSP[:, l, c, 1:2]), r=R_fb[1] + [RM("nsp")], w=R_fb[3])
            S.op("act", lambda e: e.activation(out=Rg[:], in_=Rg[:], func=AF.Exp, scale=NSP[:, l, c, 0:1]), r=R_fb[1] + [RM("nsp")], w=R_fb[1])
            S.op("act", lambda e: e.activation(out=A2[:], in_=A2[:], func=AF.Sqrt, bias=1.0, scale=-1.0), r=R_fb[3], w=R_fb[3])
            S.op("dve", lambda e: e.tensor_tensor(out=Ig[:], in0=Ig[:], in1=A2[:], op=ALU.mult), r=R_fb[4] + R_fb[3], w=R_fb[4])
            Hh = U
            S.op("dve", lambda e: e.tensor_tensor_scan(out=Hh[:], data0=Rg[:], data1=Ig[:], initial=0.0, op0=ALU.mult, op1=ALU.add),
                 r=R_fb[1] + R_fb[4], w=R_fb[0])
            G1 = A2
            S.op("dve", lambda e: e.tensor_tensor(out=G1[:], in0=XG[:], in1=XG[:], op=ALU.mult), r=R_fb[2], w=R_fb[3])
            S.op("dve", lambda e: e.tensor_scalar(out=G1[:], in0=G1[:], scalar1=0.044715, scalar2=1.0, op0=ALU.mult, op1=ALU.add), r=R_fb[3], w=R_fb[3])
            S.op("dve", lambda e: e.tensor_tensor(out=G1[:], in0=G1[:], in1=XG[:], op=ALU.mult), r=R_fb[3] + R_fb[2], w=R_fb[3])
            S.op("act", lambda e: e.activation(out=G1[:], in_=G1[:], func=AF.Sigmoid, scale=1.5957691216057308), r=R_fb[3], w=R_fb[3])
            S.op("dve", lambda e: e.tensor_tensor(out=G1[:], in0=G1[:], in1=XG[:], op=ALU.mult), r=R_fb[3] + R_fb[2], w=R_fb[3])
            S.op("dve", lambda e: e.tensor_tensor(out=YT[:, 6 + c, :], in0=Hh[:], in1=G1[:], op=ALU.mult), r=R_fb[0] + R_fb[3], w=R_yt[6 + c])

        S.barrier()
        S.op("dve", lambda e: e.memset(V1[:, :, :, 128:130], 1.0), w=[RM("v1ones")])
        S.op("dve", lambda e: e.memset(MASKA[:], 0.0), w=[RM("maska")])
        S.op("dve", lambda e: e.memset(MASKA[0:1, 64:128], 1.0), w=[RM("maska")])
        S.op("dve", lambda e: e.memset(MASKB[:], 0.0), w=[RM("maskb")])
        for jd in range(2):
            S.op("dve", lambda e, jd=jd: e.memset(MASKB[0:1, jd, jd * 128:jd * 128 + 64], -30000.0), w=[RM("maskb")])
        S.op("dve", lambda e: e.memset(QT0[64:128, :], 0.0), w=[RM("qt0_%d" % i) for i in range(4)])
        S.op("dve", lambda e: e.memset(QT1[0:64, :], 0.0), w=[RM("qt1_%d" % i) for i in range(4)])
        S.op("pool", lambda e, l=l: e.dma_start(out=WV[:], in_=w_in_d[l][:, 1024:1536].rearrange("(k p) n -> p k n", p=128)),
             w=[RM("wv")], dma_sem="wv")
        for tt in range(NT):
            b = 2 + ring("pa", 2)
            for k in range(8):
                S.op("pe", lambda e, b=b, k=k, tt=tt: e.matmul(PS[b][:], lhsT=XT[:, k, tt * 128:(tt + 1) * 128], rhs=WV[:, k, :], start=(k == 0), stop=(k == 7)),
                     r=[RM("wv"), R_xt[tt]], w=[PSR[b]])
            S.op("dve", lambda e, b=b, tt=tt: e.tensor_copy(out=V1[:, tt, :, 0:128], in_=PS[b][:].rearrange("p (h e) -> p h e", h=4)),
                 r=[PSR[b]], w=[RM("v1_%d" % tt)])
        ST_BANKS = (2, 3, 7)
        for h in range(4):
            for si, col0 in enumerate((h * 128, 512 + h * 128)):
                S.op("pool", lambda e, l=l, si=si, col0=col0: e.dma_start(
                    out=WQK[:, 0, :, si * 128:(si + 1) * 128],
                    in_=w_in_d[l][:, col0:col0 + 128].rearrange("(k p) n -> p k n", p=128)),
                    w=[RM("wqk_%d" % si)], dma_sem="wqk_%d" % si)
            for nt in range(4):
                for si in range(2):
                    b = 2 + ring("pa", 2)
                    for k in range(8):
                        S.op("pe", lambda e, b=b, k=k, nt=nt, si=si: e.matmul(PS[b][:], lhsT=WQK[:, 0, k, si * 128:(si + 1) * 128],
                                                                          rhs=XT[:, k, nt * 512:(nt + 1) * 512], start=(k == 0), stop=(k == 7)),
                             r=[RM("wqk_%d" % si)] + xt_blk(4 * nt, 4 * nt + 4), w=[PSR[b]])
                    if si == 0:
                        S.op("dve", lambda e, b=b, nt=nt: e.tensor_copy(out=QT0[0:64, nt * 512:(nt + 1) * 512], in_=PS[b][0:64, :]),
                             r=[PSR[b]], w=[RM("qt0_%d" % nt)])
                        S.op("dve", lambda e, b=b, nt=nt: e.tensor_copy(out=QT1[64:128, nt * 512:(nt + 1) * 512], in_=PS[b][64:128, :]),
                             r=[PSR[b]], w=[RM("qt1_%d" % nt)])
                    else:
                        S.op("dve", lambda e, b=b, nt=nt: e.tensor_copy(out=KT[:, nt * 512:(nt + 1) * 512], in_=PS[b][:]),
                             r=[PSR[b]], w=[RM("kt_%d" % nt)])

            def s1_groups(u, st):
                qb, m = u
                QTm = QT0 if m == 0 else QT1
                qres = RM("qt%d_%d" % (m, qb // 2))
                groups = []
                for p in range(qb + 1):
                    def g(p=p):
                        b = ST_BANKS[ring("st", 3)]
                        for j in range(2):
                            kt = 2 * p + j
                            jd = kt - 2 * qb
                            S.op("pe", lambda e, kt=kt, j=j, jd=jd: e.matmul(PS[b][:, j * 256:(j + 1) * 256], lhsT=KT[:, kt * 128:(kt + 1) * 128],
                                                                             rhs=QTm[:, qb * 256:(qb + 1) * 256], start=True, stop=(jd < 0)),
                                 r=[RM("kt_%d" % (kt // 4)), qres], w=[PSR[b]])
                            if jd >= 0:
                                S.op("pe", lambda e, j=j, jd=jd: e.matmul(PS[b][:, j * 256:(j + 1) * 256], lhsT=MASKA[:], rhs=MASKB[:, jd, :],
                                                                          start=False, stop=True),
                                     r=[RM("maska"), RM("maskb")], w=[PSR[b]])
                        S.op("act", lambda e: e.activation(out=PT[:, st, 2 * p:2 * p + 2, :], in_=PS[b][:].rearrange("p (j x) -> p j x", j=2),
                                                           func=AF.Exp, scale=0.125),
                             r=[PSR[b]], w=[RM("pt%d_%d" % (st, p))])
                    groups.append(g)
                return groups

            def s2_list(u, st):
                qb, m = u
                fl = []
                for jq in range(2):
                    last_kt = 2 * qb + jq
                    for kt in range(last_kt + 1):
                        def f(jq=jq, kt=kt, last_kt=last_kt):
                            ab = 4 * m + (qb % 2)
                            S.op("pe", lambda e: e.matmul(
                                PS[ab][:, jq * 256:jq * 256 + 129], lhsT=PT[:, st, kt, jq * 128:(jq + 1) * 128], rhs=V1[:, kt, h, 0:129],
                                start=(kt == 0), stop=(kt == last_kt)),
                                r=[RM("pt%d_%d" % (st, kt // 2)), RM("v1_%d" % kt), RM("v1ones")], w=[PSR[ab]])
                        fl.append(f)
                return fl

            def epilogue(qb):
                acc0 = PSALL[:, 0:1024].rearrange("p (j x) -> p j x", j=4)
                acc1 = PSALL[:, 2048:3072].rearrange("p (j x) -> p j x", j=4)
                a0r, a1r = [PSR[0], PSR[1]], [PSR[4], PSR[5]]
                bc = lambda ap: ap.unsqueeze(2).to_broadcast([128, 4, 128])
                S.op("dve", lambda e: e.reciprocal(out=SCR[:, 640:644], in_=acc0[:, :, 128]), r=a0r, w=[RM("ep_r1")])
                S.op("dve", lambda e: e.reciprocal(out=SCR[:, 644:648], in_=acc1[:, :, 128]), r=a1r, w=[RM("ep_r2")])
                S.op("dve", lambda e: e.tensor_scalar(out=SCR[:, 644:648], in0=SCR[:, 644:648], scalar1=LAMV[:, l:l + 1], scalar2=None, op0=ALU.mult),
                     r=[RM("ep_r2"), RM("lamv")], w=[RM("ep_r2")])
                S.op("dve", lambda e: e.tensor_tensor(out=OST[:, 0, :, :], in0=acc1[:, :, 0:128], in1=bc(SCR[:, 644:648]), op=ALU.mult),
                     r=a1r + [RM("ep_r2")], w=[RM("ost0")])
                S.op("dve", lambda e: e.tensor_tensor(out=OST[:, 1, :, :], in0=acc0[:, :, 0:128], in1=bc(SCR[:, 640:644]), op=ALU.mult),
                     r=a0r + [RM("ep_r1")], w=[RM("ost1")])
                S.op("dve", lambda e: e.tensor_tensor(out=OST[:, 1, :, :], in0=OST[:, 1, :, :], in1=OST[:, 0, :, :], op=ALU.subtract),
                     r=[RM("ost0"), RM("ost1")], w=[RM("ost1")])
                S.op("dve", lambda e: e.tensor_tensor(out=OST[:, 0, :, :], in0=OST[:, 1, :, :], in1=OST[:, 1, :, :], op=ALU.mult),
                     r=[RM("ost1")], w=[RM("ost0")])
                S.op("dve", lambda e: e.reduce_sum(out=SCR[:, 648:652], in_=OST[:, 0, :, :], axis=AX.X), r=[RM("ost0")], w=[RM("ep_ss")])
                S.op("act", lambda e: e.activation(out=SCR[:, 652:656], in_=SCR[:, 648:652], func=AF.Sqrt, bias=HN_EPS, scale=1.0 / 128),
                     r=[RM("ep_ss")], w=[RM("ep_rms")])
                S.op("dve", lambda e: e.reciprocal(out=SCR[:, 656:660], in_=SCR[:, 652:656]), r=[RM("ep_rms")], w=[RM("ep_ri")])
                S.op("dve", lambda e: e.tensor_tensor(out=OST[:, 1, :, :], in0=OST[:, 1, :, :], in1=bc(SCR[:, 656:660]), op=ALU.mult),
                     r=[RM("ost1"), RM("ep_ri")], w=[RM("ost1")])
                S.op("dve", lambda e: e.tensor_tensor(out=OST[:, 0, :, :], in0=OST[:, 1, :, :],
                                                      in1=SUBG[:, l, :].unsqueeze(1).to_broadcast([128, 4, 128]), op=ALU.mult),
                     r=[RM("ost1"), RM("subg")], w=[RM("ost0")])
                for jq in range(4):
                    S.op("pe", lambda e, jq=jq: e.transpose(out=PS[6][:, jq * 128:(jq + 1) * 128], in_=OST[:, 0, jq, :], identity=IDENT[:]),
                         r=[RM("ost0"), RM("ident")], w=[PSR[6]])
                S.op("act", lambda e: e.activation(out=YT[:, h, (qb - 1) * 256:(qb + 1) * 256], in_=PS[6][:], func=AF.Copy),
                     r=[PSR[6]], w=[R_yt[h][qb // 2]])

            units = [(qb, m) for qb in range(8) for m in range(2)]
            for g in s1_groups(units[0], 0):
                g()
            for i, u in enumerate(units):
                nxt = s1_groups(units[i + 1], (i + 1) % 2) if i + 1 < len(units) else []
                pv = s2_list(u, i % 2)
                if nxt:
                    per = -(-len(pv) // len(nxt))
                    idx = 0
                    for g in nxt:
                        g()
                        for f in pv[idx:idx + per]:
                            f()
                        idx += per
                    for f in pv[idx:]:
                        f()
                else:
                    for f in pv:
                        f()
                if u[1] == 1 and u[0] % 2 == 1:
                    epilogue(u[0])

        S.barrier()
        S.op("pool", lambda e, l=l: e.dma_start(out=WO[:], in_=w_out_d[l].rearrange("(k p) n -> p k n", p=128)), w=[RM("wo")], dma_sem="wo")
        if is_moe:
            S.op("sp", lambda e, l=l: e.dma_start(out=WR[:], in_=rt_d[l // 2].rearrange("(k p) n -> p k n", p=128)), w=[RM("wr")], dma_sem="wr")
            S.op("dve", lambda e: e.tensor_copy(out=WRH[:], in_=WR[:]), r=[RM("wr")], w=[RM("wrh")])
            S.op("dve", lambda e: e.tensor_tensor(out=WRL[:], in0=WR[:], in1=WRH[:], op=ALU.subtract), r=[RM("wr"), RM("wrh")], w=[RM("wrhl")])

        def layer_norm_group(tiles, ga, ba, gp):
            q0 = 700 + 80 * gp
            pr = "ln%d_" % gp
            for j, tt in enumerate(tiles):
                S.op("dve", lambda e, j=j, tt=tt: e.bn_stats(out=SCR[:, q0 + 12 * j:q0 + 12 * j + 6], in_=XRES[:, tt, 0:512]),
                     r=[R_xres[tt]], w=[RM(pr + "s%da" % j)])
                S.op("dve", lambda e, j=j, tt=tt: e.bn_stats(out=SCR[:, q0 + 12 * j + 6:q0 + 12 * j + 12], in_=XRES[:, tt, 512:1024]),
                     r=[R_xres[tt]], w=[RM(pr + "s%db" % j)])
            for j, tt in enumerate(tiles):
                S.op("dve", lambda e, j=j: e.bn_aggr(out=SCR[:, q0 + 48 + 2 * j:q0 + 50 + 2 * j], in_=SCR[:, q0 + 12 * j:q0 + 12 * j + 12]),
                     r=[RM(pr + "s%da" % j), RM(pr + "s%db" % j)], w=[RM(pr + "mv")])
            mv = SCR[:, q0 + 48:q0 + 56].rearrange("p (j t) -> p j t", t=2)
            S.op("act", lambda e: e.activation(out=SCR[:, q0 + 56:q0 + 60], in_=mv[:, :, 1], func=AF.Sqrt, bias=LN_EPS), r=[RM(pr + "mv")], w=[RM(pr + "sd")])
            S.op("dve", lambda e: e.reciprocal(out=SCR[:, q0 + 60:q0 + 64], in_=SCR[:, q0 + 56:q0 + 60]), r=[RM(pr + "sd")], w=[RM(pr + "rs")])
            S.op("dve", lambda e: e.scalar_tensor_tensor(out=SCR[:, q0 + 64:q0 + 68], in0=mv[:, :, 0], scalar=-1.0, in1=SCR[:, q0 + 60:q0 + 64],
                                                         op0=ALU.mult, op1=ALU.mult), r=[RM(pr + "mv"), RM(pr + "rs")], w=[RM(pr + "nm")])
            for j, tt in enumerate(tiles):
                S.op("act", lambda e, j=j, tt=tt: e.activation(out=XRES[:, tt, :], in_=XRES[:, tt, :], func=AF.Identity,
                                                               bias=SCR[:, q0 + 64 + j:q0 + 65 + j], scale=SCR[:, q0 + 60 + j:q0 + 61 + j]),
                     r=[R_xres[tt], RM(pr + "rs"), RM(pr + "nm")], w=[R_xres[tt]])
                S.op("dve", lambda e, tt=tt: e.tensor_tensor(out=XRES[:, tt, :], in0=XRES[:, tt, :], in1=LNBC[:, ga, :], op=ALU.mult),
                     r=[R_xres[tt], RM("lnbc")], w=[R_xres[tt]])
                S.op("dve", lambda e, tt=tt: e.tensor_tensor(out=XRES[:, tt, :], in0=XRES[:, tt, :], in1=LNBC[:, ba, :], op=ALU.add),
                     r=[R_xres[tt], RM("lnbc")], w=[R_xres[tt]])

        LALL = SCR[:, 0:128].rearrange("p (t e) -> p t e", e=NE)
        last_wout = [None]

        def wout_group(g):
            for tt in range(4 * g, 4 * g + 4):
                for hf in range(2):
                    b = ring("po", 4)
                    for k in range(8):
                        last_wout[0] = S.op("pe", lambda e, b=b, k=k, hf=hf, tt=tt: e.matmul(PS[b][:], lhsT=YT[:, k, tt * 128:(tt + 1) * 128], rhs=WO[:, k, hf * 512:(hf + 1) * 512],
                                                                          start=(k == 0), stop=(k == 7)),
                             r=[RM("wo"), R_yt[k][tt // 4]], w=[PSR[b]])
                    S.op("dve", lambda e, b=b, hf=hf, tt=tt: e.tensor_tensor(out=XRES[:, tt, hf * 512:(hf + 1) * 512], in0=PS[b][:],
                                                                             in1=XRES[:, tt, hf * 512:(hf + 1) * 512], op=ALU.add),
                         r=[PSR[b], R_xres[tt]], w=[R_xres[tt]])

        for g in range(4):
            wout_group(g)
        for g in range(4):
            tiles = list(range(4 * g, 4 * g + 4))
            layer_norm_group(tiles, 0, 1, g % 2)
            for tt in tiles:
                b = (4 + ring("tr", 4)) if is_moe else None
                emit_transposes(tt, router_bank=b)
                if is_moe:
                    S.op("act", lambda e, b=b, tt=tt: e.activation(out=LALL[:, tt, :], in_=PS[b][:, 0:NE], func=AF.Copy), r=[PSR[b]], w=[RM("rt_l")])
        if is_moe:
            bc8 = lambda ap: ap.unsqueeze(2).to_broadcast([128, NT, NE])
            v3 = lambda a, b_: SCR[:, a:b_].rearrange("p (t e) -> p t e", e=NE)
            M0, EQ, M1, MK, EX, SM, RD = SCR[:, 128:144], v3(144, 272), SCR[:, 272:288], v3(288, 416), v3(416, 544), SCR[:, 544:560], SCR[:, 560:576]
            S.op("dve", lambda e: e.tensor_reduce(out=M0, in_=LALL, axis=AX.X, op=ALU.max), r=[RM("rt_l")], w=[RM("rt_m0")])
            S.op("dve", lambda e: e.tensor_tensor(out=EQ, in0=LALL, in1=bc8(M0), op=ALU.is_equal), r=[RM("rt_l"), RM("rt_m0")], w=[RM("rt_eq")])
            S.op("dve", lambda e: e.scalar_tensor_tensor(out=EQ, in0=EQ, scalar=-1e30, in1=LALL, op0=ALU.mult, op1=ALU.add),
                 r=[RM("rt_eq"), RM("rt_l")], w=[RM("rt_eq")])
            S.op("dve", lambda e: e.tensor_reduce(out=M1, in_=EQ, axis=AX.X, op=ALU.max), r=[RM("rt_eq")], w=[RM("rt_m1")])
            S.op("dve", lambda e: e.tensor_tensor(out=MK, in0=LALL, in1=bc8(M1), op=ALU.is_ge), r=[RM("rt_l"), RM("rt_m1")], w=[RM("rt_mk")])
            S.op("dve", lambda e: e.tensor_tensor(out=EX, in0=LALL, in1=bc8(M0), op=ALU.subtract), r=[RM("rt_l"), RM("rt_m0")], w=[RM("rt_ex")])
            S.op("act", lambda e: e.activation(out=EX, in_=EX, func=AF.Exp), r=[RM("rt_ex")], w=[RM("rt_ex")])
            S.op("dve", lambda e: e.tensor_tensor(out=EX, in0=EX, in1=MK, op=ALU.mult), r=[RM("rt_ex"), RM("rt_mk")], w=[RM("rt_ex")])
            S.op("dve", lambda e: e.reduce_sum(out=SM, in_=EX, axis=AX.X), r=[RM("rt_ex")], w=[RM("rt_sm")])
            S.op("dve", lambda e: e.reciprocal(out=RD, in_=SM), r=[RM("rt_sm")], w=[RM("rt_rd")])
            S.op("dve", lambda e: e.tensor_tensor(out=GW[:, :, :], in0=EX, in1=bc8(RD), op=ALU.mult), r=[RM("rt_ex"), RM("rt_rd")],
                 w=[RM("gw%d" % tt) for tt in range(NT)])

        experts = list(range(_DBG.get('n_exp', NE))) if is_moe else [0]
        for ex in experts:
            if is_moe:
                wg_src, wu_src, wd_src = mg_d[l // 2][ex], mu_d[l // 2][ex], md_d[l // 2][ex]
            else:
                wg_src, wu_src, wd_src = fg_d[l // 2], fu_d[l // 2], fd_d[l // 2]
            for (c0, ncg) in GROUPS:
                sl = ring("ffw", 2)
                ncol = ncg * 128
                S.op("pool", lambda e, sl=sl, c0=c0, ncol=ncol, src=wg_src: e.dma_start(
                    out=WG[:, sl, :, 0:ncol], in_=src[:, c0 * 128:c0 * 128 + ncol].rearrange("(k p) n -> p k n", p=128)),
                    w=[RM("wg%d" % sl)], dma_sem="wg%d" % sl, extra_deps=[last_wout[0]])
                S.op("pool", lambda e, sl=sl, c0=c0, ncol=ncol, src=wu_src: e.dma_start(
                    out=WU[:, sl, :, 0:ncol], in_=src[:, c0 * 128:c0 * 128 + ncol].rearrange("(k p) n -> p k n", p=128)),
                    w=[RM("wu%d" % sl)], dma_sem="wu%d" % sl, extra_deps=[last_wout[0]])
                S.op("pool", lambda e, sl=sl, c0=c0, ncg=ncg, src=wd_src: e.dma_start(
                    out=WD[:, sl, 0:ncg, :], in_=src[c0 * 128:(c0 + ncg) * 128, :].rearrange("(c p) n -> p c n", p=128)),
                    w=[RM("wd%d" % sl)], dma_sem="wd%d" % sl, extra_deps=[last_wout[0]])
                for cc in range(ncg):
                    for nt in range(4):
                        bg = ring("fg", 2)
                        bu = 2 + ring("fu", 2)
                        for k in range(8):
                            S.op("pe", lambda e, bg=bg, k=k, cc=cc, nt=nt, sl=sl: e.matmul(
                                PS[bg][:], lhsT=WG[:, sl, k, cc * 128:(cc + 1) * 128], rhs=XT[:, k, nt * 512:(nt + 1) * 512], start=(k == 0), stop=(k == 7)),
                                r=[RM("wg%d" % sl)] + xt_blk(4 * nt, 4 * nt + 4), w=[PSR[bg]])
                        for k in range(8):
                            S.op("pe", lambda e, bu=bu, k=k, cc=cc, nt=nt, sl=sl: e.matmul(
                                PS[bu][:], lhsT=WU[:, sl, k, cc * 128:(cc + 1) * 128], rhs=XT[:, k, nt * 512:(nt + 1) * 512], start=(k == 0), stop=(k == 7)),
                                r=[RM("wu%d" % sl)] + xt_blk(4 * nt, 4 * nt + 4), w=[PSR[bu]])
                        ss = ring("sil", 2)
                        S.op("act", lambda e, bg=bg, ss=ss: e.activation(out=SIL[:, ss, :], in_=PS[bg][:], func=AF.Silu), r=[PSR[bg]], w=[RM("sil%d" % ss)])
                        S.op("dve", lambda e, bu=bu, ss=ss, cc=cc, nt=nt, sl=sl: e.tensor_tensor(
                            out=ACTT[:, sl, cc, nt * 512:(nt + 1) * 512], in0=PS[bu][:], in1=SIL[:, ss, :], op=ALU.mult),
                            r=[PSR[bu], RM("sil%d" % ss)], w=[RM("actt%d_%d_%d" % (sl, cc, nt))])
                for tt in range(NT):
                    b0 = 4 + 2 * ring("fd", 2)
                    for cc in range(ncg):
                        for hf in range(2):
                            S.op("pe", lambda e, b0=b0, cc=cc, hf=hf, tt=tt, sl=sl, ncg=ncg: e.matmul(
                                PS[b0 + hf][:], lhsT=ACTT[:, sl, cc, tt * 128:(tt + 1) * 128], rhs=WD[:, sl, cc, hf * 512:(hf + 1) * 512],
                                start=(cc == 0), stop=(cc == ncg - 1)),
                                r=[RM("wd%d" % sl), RM("actt%d_%d_%d" % (sl, cc, tt // 4))], w=[PSR[b0 + hf]])
                    for hf in range(2):
                        if is_moe:
                            S.op("dve", lambda e, b0=b0, hf=hf, tt=tt, ex=ex: e.scalar_tensor_tensor(
                                out=XRES[:, tt, hf * 512:(hf + 1) * 512], in0=PS[b0 + hf][:], scalar=GW[:, tt, ex:ex + 1],
                                in1=XRES[:, tt, hf * 512:(hf + 1) * 512], op0=ALU.mult, op1=ALU.add),
                                r=[PSR[b0 + hf], R_xres[tt], RM("gw%d" % tt)], w=[R_xres[tt]])
                        else:
                            S.op("dve", lambda e, b0=b0, hf=hf, tt=tt: e.tensor_tensor(
                                out=XRES[:, tt, hf * 512:(hf + 1) * 512], in0=PS[b0 + hf][:], in1=XRES[:, tt, hf * 512:(hf + 1) * 512], op=ALU.add),
                                r=[PSR[b0 + hf], R_xres[tt]], w=[R_xres[tt]])

        for g in range(4):
            tiles = list(range(4 * g, 4 * g + 4))
            layer_norm_group(tiles, 2, 3, g % 2)
            if not last:
                for tt in tiles:
                    emit_transposes(tt)

    for q in range(4):
        o = S.op("sp", lambda e, q=q: e.dma_start(
            out=out_d[512 * q:512 * (q + 1), :].rearrange("(t p) d -> p t d", p=128), in_=XRES[:, 4 * q:4 * q + 4, :]),
            r=R_xres[4 * q:4 * q + 4], dma_sem="out%d" % q)
        S.final.append(o)
    S.emit(nc)
    return nc


def _prep_shared(inp):
    f = lambda a: np.ascontiguousarray(np.asarray(a, dtype=np.float32))
    bdw = np.zeros((DEPTH, 6, 128, 128), np.float32)
    for l in range(DEPTH):
        for si, key in enumerate(("pool_w", "lru_wa", "lru_wi")):
            w = np.asarray(inp[key][l], np.float32)
            for c in range(2):
                for hf in range(2):
                    bdw[l, 2 * si + c, 64 * hf:64 * hf + 64, 64 * hf:64 * hf + 64] = w[2 * c + hf]
    pp = np.zeros((128, DEPTH, 18), np.float32)
    for l in range(DEPTH):
        for c in range(2):
            sl = slice(128 * c, 128 * c + 128)
            for j in range(4):
                pp[:, l, c * 9 + j] = np.asarray(inp["conv_w"][l][j][sl])
            pp[:, l, c * 9 + 4] = np.asarray(inp["conv_b"][l][sl])
            pp[:, l, c * 9 + 5] = np.asarray(inp["lru_ba"][l][sl])
            pp[:, l, c * 9 + 6] = np.asarray(inp["lru_bi"][l][sl])
            pp[:, l, c * 9 + 7] = np.asarray(inp["lru_lambda"][l][sl])
            pp[:, l, c * 9 + 8] = np.asarray(inp["pool_scale"][l][sl])
    lnp = np.stack([np.asarray(inp[k], np.float32) for k in ("ln1_g", "ln1_b", "ln2_g", "ln2_b")], axis=1)
    shared = {
        "w_in": f(inp["w_in"]), "w_out": f(inp["w_out"]),
        "attn_lambda": f(inp["attn_lambda"]).reshape(-1), "attn_subln_g": f(inp["attn_subln_g"]).reshape(-1),
        "bdw": bdw, "pp": pp, "lnp": f(lnp),
        "ffn_w_gate": f(inp["ffn_w_gate"]), "ffn_w_up": f(inp["ffn_w_up"]), "ffn_w_down": f(inp["ffn_w_down"]),
        "router_w": f(inp["router_w"]), "moe_w_gate": f(inp["moe_w_gate"]), "moe_w_up": f(inp["moe_w_up"]),
        "moe_w_down": f(inp["moe_w_down"]), "ident": np.eye(128, dtype=np.float32),
    }
    return shared


def kernel(**inputs):
    x = np.asarray(inputs["x"], np.float32)
    shared = _prep_shared(inputs)
    nc = build_program(DEPTH)
    in_maps = []
    for b in range(8):
        m = dict(shared)
        m["x"] = np.ascontiguousarray(x[b])
        in_maps.append(m)
    res = run_bass_kernel_spmd(nc, in_maps, core_ids=list(range(8)))
    return np.stack([np.asarray(r["out"], np.float32) for r in res.results], axis=0)
```

```python
import math
from contextlib import ExitStack
import numpy as np
import concourse.bass as bass
import concourse.mybir as mybir
from concourse.bass_utils import run_bass_kernel_spmd

F32 = mybir.dt.float32
BF16 = mybir.dt.bfloat16
AF = mybir.ActivationFunctionType
ALU = mybir.AluOpType
AX = mybir.AxisListType

D = 1024
S_LEN = 2048
DEPTH = 4
NT = 16
DFF = 2816
NE = 8
ALPHA = (2.0 * DEPTH) ** 0.25
LN_EPS = 1e-5
HN_EPS = 1e-5
POOL_W = (2, 4, 8, 16)
GROUPS = [(0, 4), (4, 4), (8, 4), (12, 4), (16, 4), (20, 2)]


class Res:
    __slots__ = ("name", "w", "rs", "rd")

    def __init__(self, name):
        self.name = name
        self.w = None
        self.rs = {}
        self.rd = []


class Op:
    __slots__ = ("eng", "fn", "deps", "signal", "done", "is_dma", "sem")


class _Rec:
    def __init__(self):
        self.call = None

    def __getattr__(self, name):
        def f(*a, **k):
            self.call = (name, a, k)
            return self
        return f


class Sched:
    ENGS = ("pe", "act", "dve", "pool", "sp")

    def __init__(self):
        self.ops = {e: [] for e in self.ENGS}
        self.dmas_since_barrier = []
        self.final = []

    def op(self, eng, fn, r=(), w=(), dma_sem=None, extra_deps=()):
        o = Op()
        rec = _Rec()
        fn(rec)
        o.eng, o.fn, o.deps, o.signal, o.done = eng, rec.call, [], False, None
        o.is_dma, o.sem = dma_sem is not None, dma_sem
        deps = {}
        for res in r:
            if res.w is not None:
                deps[res.w] = True
        for res in w:
            if res.w is not None and res.w not in deps:
                deps[res.w] = False
            for rd in res.rs.values():
                if rd not in deps:
                    deps[rd] = False
            for rd in res.rd:
                if rd not in deps:
                    deps[rd] = False
        for d in extra_deps:
            deps[d] = True
        for d, raw in deps.items():
            if d is o:
                continue
            if (not d.is_dma) and d.eng == eng and not raw and eng == "pe":
                continue
            o.deps.append(d)
            d.signal = True
        for res in r:
            if o.is_dma:
                res.rd.append(o)
            else:
                res.rs[eng] = o
        for res in w:
            res.w = o
            res.rs = {}
            res.rd = []
        if o.is_dma:
            o.signal = True
            self.dmas_since_barrier.append(o)
        self.ops[eng].append(o)
        return o

    def barrier(self):
        lasts = []
        for e in self.ENGS:
            for o in reversed(self.ops[e]):
                if not o.is_dma and o.fn is not None:
                    lasts.append(o)
                    break
        deps = lasts + self.dmas_since_barrier
        self.dmas_since_barrier = []
        for e in self.ENGS:
            o = Op()
            o.eng, o.fn, o.deps, o.signal, o.done, o.is_dma, o.sem = e, None, [], False, None, False, None
            for d in deps:
                if d.eng == e and not d.is_dma and e == "pe":
                    continue
                o.deps.append(d)
                d.signal = True
            self.ops[e].append(o)

    def emit(self, nc):
        dma_counts = {}
        sem_names = set()
        for e in self.ENGS:
            cnt = 0
            for o in self.ops[e]:
                if o.is_dma:
                    c = dma_counts.get(o.sem, 0) + 16
                    dma_counts[o.sem] = c
                    o.done = ("d_" + o.sem, c)
                    sem_names.add(o.done[0])
                elif o.signal:
                    cnt += 1
                    o.done = ("e_" + e, cnt)
                    sem_names.add(o.done[0])
        engobj = {"pe": nc.tensor, "act": nc.scalar, "dve": nc.vector, "pool": nc.gpsimd, "sp": nc.sync}
        with ExitStack() as st:
            sems = {n: st.enter_context(nc.semaphore(n)) for n in sorted(sem_names)}
            block = st.enter_context(nc.Block())

            def run(ename):
                def body(e):
                    waited = {}
                    for o in self.ops[ename]:
                        need = {}
                        for d in o.deps:
                            sname, val = d.done
                            if need.get(sname, 0) < val:
                                need[sname] = val
                        for sname, val in need.items():
                            if waited.get(sname, 0) >= val:
                                continue
                            e.wait_ge(sems[sname], val)
                            waited[sname] = val
                        if o.fn is None:
                            continue
                        name_, a_, k_ = o.fn
                        ins = getattr(e, name_)(*a_, **k_)
                        if o.signal:
                            ins.then_inc(sems[o.done[0]], 16 if o.is_dma else 1)
                    if ename == "sp":
                        for o in self.final:
                            e.wait_ge(sems[o.done[0]], o.done[1])
                return body

            block.tensor(run("pe"))
            block.scalar(run("act"))
            block.vector(run("dve"))
            block.gpsimd(run("pool"))
            block.sync(run("sp"))


_DBG = {}


def build_program(n_layers=DEPTH):
    nc = bass.Bass("TRN2", target_bir_lowering=False)
    S = Sched()

    def din(name, shape):
        return nc.dram_tensor(name, list(shape), F32, kind="ExternalInput").ap()

    x_d = din("x", [S_LEN, D])
    w_in_d = din("w_in", [DEPTH, D, 2304])
    w_out_d = din("w_out", [DEPTH, D, D])
    lam_d = din("attn_lambda", [DEPTH * 4 * 64])
    subg_d = din("attn_subln_g", [DEPTH * 128])
    bdw_d = din("bdw", [DEPTH, 6, 128, 128])
    pp_d = din("pp", [128, DEPTH, 18])
    ln_d = din("lnp", [DEPTH, 4, D])
    fg_d = din("ffn_w_gate", [2, D, DFF])
    fu_d = din("ffn_w_up", [2, D, DFF])
    fd_d = din("ffn_w_down", [2, DFF, D])
    rt_d = din("router_w", [2, D, NE])
    mg_d = din("moe_w_gate", [2, NE, D, DFF])
    mu_d = din("moe_w_up", [2, NE, D, DFF])
    md_d = din("moe_w_down", [2, NE, DFF, D])
    ident_d = din("ident", [128, 128])
    out_d = nc.dram_tensor("out", [S_LEN, D], F32, kind="ExternalOutput").ap()

    cur = [16384]

    def alloc(name, shape, dt, at=None):
        nbytes = int(np.prod(shape[1:])) * (4 if dt == F32 else 2)
        nbytes = (nbytes + 63) // 64 * 64
        if at is None:
            off = cur[0]
            cur[0] += nbytes
        else:
            off = at
        assert off + nbytes <= 16384 + 212800, (name, off, nbytes)
        return nc.alloc_sbuf_tensor_at(name, list(shape), dt, offset=off), off + nbytes

    XRES, _ = alloc("xres", [128, NT, D], F32)
    XT, _ = alloc("xt", [128, 8, S_LEN], BF16)
    LNBC, _ = alloc("lnbc", [128, 4, D], F32)
    IDENT, _ = alloc("ident_sb", [128, 128], F32)
    PP, _ = alloc("pp_sb", [128, DEPTH, 18], F32)
    NSP, _ = alloc("nsp", [128, DEPTH, 2, 2], F32)
    LAMV, _ = alloc("lamv", [128, 16], F32)
    SUBG, _ = alloc("subg", [128, DEPTH, 128], F32)
    INVW, _ = alloc("invw", [128, 2], F32)
    INVC, _ = alloc("invc", [128, 2, 16], F32)
    BDW, _ = alloc("bdw_sb", [128, 2, 6, 128], BF16)
    GW, _ = alloc("gw", [128, NT, NE], F32)
    SCR, scr_end = alloc("scr", [128, 1024], F32)
    scr0 = scr_end - 4096
    MASKB, _ = alloc("maskb", [128, 2, 256], BF16, at=scr0)
    MASKA, _ = alloc("maska", [128, 128], BF16, at=scr0 + 864 * 4)
    WORK0 = cur[0]

    def walloc(name, shape, dt, off):
        t, end = alloc(name, shape, dt, at=WORK0 + off)
        return t, end - WORK0

    YT, o1 = walloc("yt", [128, 8, S_LEN], BF16, 0)
    FB = []
    o = o1
    for i in range(5):
        t, o = walloc("fb%d" % i, [128, S_LEN], F32, o)
        FB.append(t)
    HB, o = walloc("hb", [128, S_LEN], BF16, o)
    WA, o = walloc("wa", [128, 8, 384], BF16, o)
    LAMB, _ = walloc("lamb", [128, DEPTH * 256], F32, o1)
    V1, o = walloc("v1", [128, NT, 4, 130], BF16, o1)
    QT0, o = walloc("qt0", [128, S_LEN], BF16, o)
    QT1, o = walloc("qt1", [128, S_LEN], BF16, o)
    KT, o = walloc("kt", [128, S_LEN], BF16, o)
    PT, o = walloc("pt", [128, 2, 16, 256], BF16, o)
    WQK, o = walloc("wqk", [128, 1, 8, 256], BF16, o)
    OST, o = walloc("ost", [128, 2, 4, 128], F32, o)
    WV, _ = walloc("wv", [128, 8, 512], BF16, o1 + 16640 + 3 * 4096)
    WO, _ = walloc("wo", [128, 8, D], BF16, o1)
    ACTT, o = walloc("actt", [128, 2, 4, S_LEN], BF16, 0)
    WG, o = walloc("wg", [128, 2, 8, 512], BF16, o)
    WU, o = walloc("wu", [128, 2, 8, 512], BF16, o)
    WD, o = walloc("wd", [128, 2, 4, D], BF16, o)
    SIL, o = walloc("sil", [128, 2, 512], BF16, o)
    WR, o = walloc("wr", [128, 8, NE], F32, o)
    XLO, o = walloc("xlo", [128, 8, 128], BF16, o)
    WRH, o = walloc("wrh", [128, 8, NE], BF16, o)
    WRL, o = walloc("wrl", [128, 8, NE], BF16, o)

    PSALL = nc.alloc_psum_tensor("psall", [128, 4096], F32)
    PS = [PSALL[:, i * 512:(i + 1) * 512] for i in range(8)]
    PSR = [Res("ps%d" % i) for i in range(8)]

    R_xres = [Res("xres%d" % i) for i in range(NT)]
    R_xt = [Res("xt%d" % i) for i in range(NT)]
    R_yt = [[Res("yt%d_%d" % (c, i)) for i in range(4)] for c in range(8)]
    R_fb = [[Res("fb%d_%d" % (b, i)) for i in range(4)] for b in range(5)]
    R_hb = [Res("hb%d" % i) for i in range(4)]
    R_misc = {}

    def RM(name):
        if name not in R_misc:
            R_misc[name] = Res(name)
        return R_misc[name]

    def xt_blk(n0, n1):
        return R_xt[n0:n1]

    ring_ctr = {}

    def ring(name, n):
        c = ring_ctr.get(name, 0)
        ring_ctr[name] = c + 1
        return c % n

    S.op("sp", lambda e: e.dma_start(out=IDENT[:], in_=ident_d), w=[RM("ident")], dma_sem="ident")
    S.op("sp", lambda e: e.dma_start(out=PP[:], in_=pp_d), w=[RM("pp")], dma_sem="pp")
    for q in range(4):
        S.op("sp", lambda e, q=q: e.dma_start(
            out=XRES[:, 4 * q:4 * q + 4, :],
            in_=x_d[512 * q:512 * (q + 1), :].rearrange("(t p) d -> p t d", p=128)),
            w=R_xres[4 * q:4 * q + 4], dma_sem="xin%d" % q)
    S.op("sp", lambda e: e.dma_start(out=LAMB[:], in_=lam_d.partition_broadcast(128)), w=[RM("lamb")], dma_sem="lamb")
    S.op("sp", lambda e: e.dma_start(out=SUBG[:].rearrange("p l e -> p (l e)"), in_=subg_d.partition_broadcast(128)),
         w=[RM("subg")], dma_sem="subg")

    lamb4 = LAMB[:].rearrange("p (l j e) -> p l j e", l=DEPTH, j=4)
    prod = SCR[:, 0:512].rearrange("p (l j e) -> p l j e", l=DEPTH, j=2)
    S.op("dve", lambda e: e.tensor_tensor(out=prod[:, :, 0, :], in0=lamb4[:, :, 0, :], in1=lamb4[:, :, 1, :], op=ALU.mult),
         r=[RM("lamb")], w=[RM("scr_a")])
    S.op("dve", lambda e: e.tensor_tensor(out=prod[:, :, 1, :], in0=lamb4[:, :, 2, :], in1=lamb4[:, :, 3, :], op=ALU.mult),
         r=[RM("lamb")], w=[RM("scr_b")])
    S.op("dve", lambda e: e.reduce_sum(out=SCR[:, 512:520], in_=SCR[:, 0:512].rearrange("p (k e) -> p k e", e=64), axis=AX.X),
         r=[RM("scr_a"), RM("scr_b")], w=[RM("scr_c")])
    S.op("act", lambda e: e.activation(out=SCR[:, 520:528], in_=SCR[:, 512:520], func=AF.Exp), r=[RM("scr_c")], w=[RM("scr_d")])
    ex8 = SCR[:, 520:528].rearrange("p (l j) -> p l j", j=2)
    S.op("dve", lambda e: e.tensor_tensor(out=LAMV[:, 0:DEPTH], in0=ex8[:, :, 0], in1=ex8[:, :, 1], op=ALU.subtract),
         r=[RM("scr_d")], w=[RM("lamv")])
    for l in range(DEPTH):
        li = 0.8 - 0.6 * math.exp(-0.3 * l)
        S.op("dve", lambda e, l=l, li=li: e.tensor_scalar(out=LAMV[:, l:l + 1], in0=LAMV[:, l:l + 1], scalar1=li, scalar2=None,
                                                           op0=ALU.add), r=[RM("lamv")], w=[RM("lamv")])
        S.op("dve", lambda e, l=l, li=li: e.tensor_scalar(out=SUBG[:, l, :], in0=SUBG[:, l, :], scalar1=1.0 - li, scalar2=None,
                                                           op0=ALU.mult), r=[RM("subg")], w=[RM("subg")])
    for c in range(2):
        lamcol = PP[:, :, c * 9 + 7]
        S.op("act", lambda e, lamcol=lamcol, c=c: e.activation(out=SCR[:, 528 + 4 * c:532 + 4 * c], in_=lamcol, func=AF.Exp, scale=-1.0),
             r=[RM("pp")], w=[RM("scr_e%d" % c)])
        S.op("act", lambda e, c=c: e.activation(out=SCR[:, 536 + 4 * c:540 + 4 * c], in_=SCR[:, 528 + 4 * c:532 + 4 * c], func=AF.Ln, bias=1.0),
             r=[RM("scr_e%d" % c)], w=[RM("scr_f%d" % c)])
        S.op("dve", lambda e, c=c: e.tensor_scalar(out=NSP[:, :, c, 0], in0=SCR[:, 536 + 4 * c:540 + 4 * c], scalar1=-8.0, scalar2=None, op0=ALU.mult),
             r=[RM("scr_f%d" % c)], w=[RM("nsp")])
        S.op("dve", lambda e, c=c: e.tensor_scalar(out=NSP[:, :, c, 1], in0=SCR[:, 536 + 4 * c:540 + 4 * c], scalar1=-16.0, scalar2=None, op0=ALU.mult),
             r=[RM("scr_f%d" % c)], w=[RM("nsp")])
    for c in range(2):
        for hf in range(2):
            wdw = POOL_W[2 * c + hf]
            ps_ = slice(64 * hf, 64 * hf + 64)
            S.op("dve", lambda e, c=c, ps_=ps_, wdw=wdw: e.memset(INVW[ps_, c:c + 1], 1.0 / wdw), w=[RM("invw")])
            S.op("dve", lambda e, c=c, ps_=ps_, wdw=wdw: e.memset(INVC[ps_, c, :], 1.0 / wdw), w=[RM("invc")])
            for t in range(min(wdw - 1, 16)):
                S.op("dve", lambda e, c=c, ps_=ps_, t=t: e.memset(INVC[ps_, c, t:t + 1], 1.0 / (t + 1)), w=[RM("invc")])
    S.barrier()

    def emit_transposes(tt, router_bank=None, xscale=1.0 / ALPHA):
        for g4 in range(2):
            b = 4 + ring("tr", 4)
            for kk in range(4):
                k = 4 * g4 + kk
                S.op("pe", lambda e, b=b, kk=kk, k=k: e.transpose(out=PS[b][:, kk * 128:(kk + 1) * 128],
                                                                in_=XRES[:, tt, k * 128:(k + 1) * 128], identity=IDENT[:]),
                     r=[R_xres[tt], RM("ident")], w=[PSR[b]])
            S.op("act", lambda e, b=b, g4=g4: e.activation(
                out=XT[:, 4 * g4:4 * g4 + 4, tt * 128:(tt + 1) * 128],
                in_=PS[b][:].rearrange("p (k t) -> p k t", k=4), func=AF.Identity, scale=xscale),
                r=[PSR[b]], w=[R_xt[tt]])
            if router_bank is not None:
                S.op("dve", lambda e, b=b, g4=g4: e.scalar_tensor_tensor(
                    out=XLO[:, 4 * g4:4 * g4 + 4, :], in0=PS[b][:].rearrange("p (k t) -> p k t", k=4), scalar=1.0 / ALPHA,
                    in1=XT[:, 4 * g4:4 * g4 + 4, tt * 128:(tt + 1) * 128], op0=ALU.mult, op1=ALU.subtract),
                    r=[PSR[b], R_xt[tt]], w=[RM("xlo%d" % g4)])
        if router_bank is not None:
            n = 0
            for k in range(8):
                for (lh, lres, rh) in ((XT[:, k, tt * 128:(tt + 1) * 128], R_xt[tt], WRH), (XLO[:, k, :], RM("xlo%d" % (k // 4)), WRH),
                                       (XT[:, k, tt * 128:(tt + 1) * 128], R_xt[tt], WRL)):
                    S.op("pe", lambda e, lh=lh, rh=rh, k=k, n=n: e.matmul(PS[router_bank][:, 0:NE], lhsT=lh, rhs=rh[:, k, :],
                                                                        start=(n == 0), stop=(n == 23)),
                         r=[RM("wrhl"), lres], w=[PSR[router_bank]])
                    n += 1

    for tt in range(NT):
        emit_transposes(tt, xscale=1.0)

    for l in range(n_layers):
        is_moe = (l % 2 == 1) and not _DBG.get('force_dense')
        last = (l == n_layers - 1)
        S.barrier()
        bs = l % 2
        S.op("pool", lambda e, l=l, bs=bs: e.dma_start(out=BDW[:, bs, :, :], in_=bdw_d[l].rearrange("s k m -> k s m")),
             w=[RM("bdw%d" % bs)], dma_sem="bdw%d" % bs)
        S.op("sp", lambda e, l=l: e.dma_start(out=LNBC[:].rearrange("p a d -> p (a d)"),
                                              in_=ln_d[l].rearrange("a d -> (a d)").partition_broadcast(128)),
             w=[RM("lnbc")], dma_sem="lnbc")
        for a in range(4):
            if last and a >= 2:
                continue
            S.op("act", lambda e, a=a: e.activation(out=LNBC[:, a, :], in_=LNBC[:, a, :], func=AF.Identity, scale=ALPHA),
                 r=[RM("lnbc")], w=[RM("lnbc")])

        for c in range(2):
            ppc = lambda j, l=l, c=c: PP[:, l, c * 9 + j:c * 9 + j + 1]
            for si, col0 in enumerate((1536 + 128 * c, 1792 + 128 * c, 2048 + 128 * c)):
                S.op("pool", lambda e, l=l, si=si, col0=col0: e.dma_start(
                    out=WA[:, :, si * 128:(si + 1) * 128],
                    in_=w_in_d[l][:, col0:col0 + 128].rearrange("(k p) n -> p k n", p=128)),
                    w=[RM("wa%d" % si)], dma_sem="wa%d" % si)
            U, XR, XG, A_, B_ = FB[0], FB[1], FB[2], FB[3], FB[4]
            for si, dst, dr in ((0, U, 0), (1, XR, 1), (2, XG, 2)):
                for nt in range(4):
                    b = ring("pa", 4)
                    for k in range(8):
                        S.op("pe", lambda e, b=b, si=si, k=k, nt=nt: e.matmul(
                            PS[b][:], lhsT=WA[:, k, si * 128:(si + 1) * 128], rhs=XT[:, k, nt * 512:(nt + 1) * 512],
                            start=(k == 0), stop=(k == 7)),
                            r=[RM("wa%d" % si)] + xt_blk(4 * nt, 4 * nt + 4), w=[PSR[b]])
                    S.op("act", lambda e, b=b, dst=dst, nt=nt: e.activation(out=dst[:, nt * 512:(nt + 1) * 512], in_=PS[b][:], func=AF.Copy),
                         r=[PSR[b]], w=[R_fb[dr][nt]])
            def shift_add(dst, di, src, si_, k):
                S.op("dve", lambda e: e.tensor_tensor(out=dst[:, k:], in0=src[:, k:], in1=src[:, :S_LEN - k], op=ALU.add),
                     r=R_fb[si_], w=R_fb[di])
                S.op("act", lambda e: e.activation(out=dst[:, 0:k], in_=src[:, 0:k], func=AF.Copy), r=R_fb[si_], w=R_fb[di])
            shift_add(A_, 3, U, 0, 1)
            shift_add(B_, 4, A_, 3, 2)
            if c == 1:
                shift_add(A_, 3, B_, 4, 4)
                shift_add(B_, 4, A_, 3, 8)
            for hf, ws, wr in ((0, A_, 3), (1, B_, 4)):
                ps_ = slice(64 * hf, 64 * hf + 64)
                S.op("dve", lambda e, ps_=ps_, ws=ws: e.scalar_tensor_tensor(
                    out=HB[ps_, 16:], in0=ws[ps_, 16:], scalar=INVW[ps_, c:c + 1], in1=U[ps_, 16:], op0=ALU.mult, op1=ALU.subtract),
                    r=R_fb[wr] + R_fb[0] + [RM("invw")], w=R_hb)
                S.op("dve", lambda e, ps_=ps_, ws=ws, hf=hf: e.tensor_tensor(out=SCR[ps_, 600 + 16 * hf:616 + 16 * hf], in0=ws[ps_, 0:16], in1=INVC[ps_, c, :], op=ALU.mult),
                     r=R_fb[wr] + [RM("invc")], w=[RM("scr_p%d" % hf)])
                S.op("dve", lambda e, ps_=ps_, hf=hf: e.tensor_tensor(out=HB[ps_, 0:16], in0=SCR[ps_, 600 + 16 * hf:616 + 16 * hf], in1=U[ps_, 0:16], op=ALU.subtract),
                     r=R_fb[0] + [RM("scr_p%d" % hf)], w=R_hb)
            for nt in range(4):
                b = 4 + ring("pb", 4)
                S.op("pe", lambda e, b=b, nt=nt: e.matmul(PS[b][:], lhsT=BDW[:, bs, c, :], rhs=HB[:, nt * 512:(nt + 1) * 512], start=True, stop=True),
                     r=[RM("bdw%d" % bs)] + R_hb, w=[PSR[b]])
                S.op("act", lambda e, b=b, nt=nt: e.activation(out=YT[:, 4 + c, nt * 512:(nt + 1) * 512], in_=PS[b][:], func=AF.Identity, scale=ppc(8)),
                     r=[PSR[b], RM("pp")], w=[R_yt[4 + c][nt]])
            XC = A_
            S.op("dve", lambda e: e.tensor_scalar(out=XC[:], in0=XR[:], scalar1=ppc(3), scalar2=ppc(4), op0=ALU.mult, op1=ALU.add),
                 r=R_fb[1] + [RM("pp")], w=R_fb[3])
            for j, sh in ((2, 1), (1, 2), (0, 3)):
                S.op("dve", lambda e, j=j, sh=sh: e.scalar_tensor_tensor(
                    out=XC[:, sh:], in0=XR[:, :S_LEN - sh], scalar=ppc(j), in1=XC[:, sh:], op0=ALU.mult, op1=ALU.add),
                    r=R_fb[1] + R_fb[3] + [RM("pp")], w=R_fb[3])
            S.op("act", lambda e: e.activation(out=HB[:], in_=XC[:], func=AF.Copy), r=R_fb[3], w=R_hb)
            Rg, Ig = XR, B_
            for (wi_, dst, dr, bcol) in ((2 + c, Rg, 1, 5), (4 + c, Ig, 4, 6)):
                for nt in range(4):
                    b = 4 + ring("pb", 4)
                    S.op("pe", lambda e, b=b, nt=nt, wi_=wi_: e.matmul(PS[b][:], lhsT=BDW[:, bs, wi_, :], rhs=HB[:, nt * 512:(nt + 1) * 512], start=True, stop=True),
                         r=[RM("bdw%d" % bs)] + R_hb, w=[PSR[b]])
                    S.op("act", lambda e, b=b, nt=nt, dst=dst, bcol=bcol: e.activation(out=dst[:, nt * 512:(nt + 1) * 512], in_=PS[b][:], func=AF.Sigmoid, bias=ppc(bcol)),
                         r=[PSR[b], RM("pp")], w=[R_fb[dr][nt]])
            S.op("dve", lambda e: e.tensor_tensor(out=Ig[:], in0=Ig[:], in1=XC[:], op=ALU.mult), r=R_fb[4] + R_fb[3], w=R_fb[4])
            A2 = XC
            S.op("act", lambda e: e.activation(out=A2[:], in_=Rg[:], func=AF.Exp, scale=NSP[:, l, c, 1:2]), r=R_fb[1] + [RM("nsp")], w=R_fb[3])
            S.op("act", lambda e: e.activation(out=Rg[:], in_=Rg[:], func=AF.Exp, scale=NSP[:, l, c, 0:1]), r=R_fb[1] + [RM("nsp")], w=R_fb[1])
            S.op("act", lambda e: e.activation(out=A2[:], in_=A2[:], func=AF.Sqrt, bias=1.0, scale=-1.0), r=R_fb[3], w=R_fb[3])
            S.op("dve", lambda e: e.tensor_tensor(out=Ig[:], in0=Ig[:], in1=A2[:], op=ALU.mult), r=R_fb[4] + R_fb[3], w=R_fb[4])
            Hh = U
            S.op("dve", lambda e: e.tensor_tensor_scan(out=Hh[:], data0=Rg[:], data1=Ig[:], initial=0.0, op0=ALU.mult, op1=ALU.add),
                 r=R_fb[1] + R_fb[4], w=R_fb[0])
            G1 = A2
            S.op("dve", lambda e: e.tensor_tensor(out=G1[:], in0=XG[:], in1=XG[:], op=ALU.mult), r=R_fb[2], w=R_fb[3])
            S.op("dve", lambda e: e.tensor_scalar(out=G1[:], in0=G1[:], scalar1=0.044715, scalar2=1.0, op0=ALU.mult, op1=ALU.add), r=R_fb[3], w=R_fb[3])
            S.op("dve", lambda e: e.tensor_tensor(out=G1[:], in0=G1[:], in1=XG[:], op=ALU.mult), r=R_fb[3] + R_fb[2], w=R_fb[3])
            S.op("act", lambda e: e.activation(out=G1[:], in_=G1[:], func=AF.Sigmoid, scale=1.5957691216057308), r=R_fb[3], w=R_fb[3])
            S.op("dve", lambda e: e.tensor_tensor(out=G1[:], in0=G1[:], in1=XG[:], op=ALU.mult), r=R_fb[3] + R_fb[2], w=R_fb[3])
            S.op("dve", lambda e: e.tensor_tensor(out=YT[:, 6 + c, :], in0=Hh[:], in1=G1[:], op=ALU.mult), r=R_fb[0] + R_fb[3], w=R_yt[6 + c])

        S.barrier()
        S.op("dve", lambda e: e.memset(V1[:, :, :, 128:130], 1.0), w=[RM("v1ones")])
        S.op("dve", lambda e: e.memset(MASKA[:], 0.0), w=[RM("maska")])
        S.op("dve", lambda e: e.memset(MASKA[0:1, 64:128], 1.0), w=[RM("maska")])
        S.op("dve", lambda e: e.memset(MASKB[:], 0.0), w=[RM("maskb")])
        for jd in range(2):
            S.op("dve", lambda e, jd=jd: e.memset(MASKB[0:1, jd, jd * 128:jd * 128 + 64], -30000.0), w=[RM("maskb")])
        S.op("dve", lambda e: e.memset(QT0[64:128, :], 0.0), w=[RM("qt0_%d" % i) for i in range(4)])
        S.op("dve", lambda e: e.memset(QT1[0:64, :], 0.0), w=[RM("qt1_%d" % i) for i in range(4)])
        S.op("pool", lambda e, l=l: e.dma_start(out=WV[:], in_=w_in_d[l][:, 1024:1536].rearrange("(k p) n -> p k n", p=128)),
             w=[RM("wv")], dma_sem="wv")
        for tt in range(NT):
            b = 2 + ring("pa", 2)
            for k in range(8):
                S.op("pe", lambda e, b=b, k=k, tt=tt: e.matmul(PS[b][:], lhsT=XT[:, k, tt * 128:(tt + 1) * 128], rhs=WV[:, k, :], start=(k == 0), stop=(k == 7)),
                     r=[RM("wv"), R_xt[tt]], w=[PSR[b]])
            S.op("dve", lambda e, b=b, tt=tt: e.tensor_copy(out=V1[:, tt, :, 0:128], in_=PS[b][:].rearrange("p (h e) -> p h e", h=4)),
                 r=[PSR[b]], w=[RM("v1_%d" % tt)])
        ST_BANKS = (2, 3, 7)
        for h in range(4):
            for si, col0 in enumerate((h * 128, 512 + h * 128)):
                S.op("pool", lambda e, l=l, si=si, col0=col0: e.dma_start(
                    out=WQK[:, 0, :, si * 128:(si + 1) * 128],
                    in_=w_in_d[l][:, col0:col0 + 128].rearrange("(k p) n -> p k n", p=128)),
                    w=[RM("wqk_%d" % si)], dma_sem="wqk_%d" % si)
            for nt in range(4):
                for si in range(2):
                    b = 2 + ring("pa", 2)
                    for k in range(8):
                        S.op("pe", lambda e, b=b, k=k, nt=nt, si=si: e.matmul(PS[b][:], lhsT=WQK[:, 0, k, si * 128:(si + 1) * 128],
                                                                          rhs=XT[:, k, nt * 512:(nt + 1) * 512], start=(k == 0), stop=(k == 7)),
                             r=[RM("wqk_%d" % si)] + xt_blk(4 * nt, 4 * nt + 4), w=[PSR[b]])
                    if si == 0:
                        S.op("dve", lambda e, b=b, nt=nt: e.tensor_copy(out=QT0[0:64, nt * 512:(nt + 1) * 512], in_=PS[b][0:64, :]),
                             r=[PSR[b]], w=[RM("qt0_%d" % nt)])
                        S.op("dve", lambda e, b=b, nt=nt: e.tensor_copy(out=QT1[64:128, nt * 512:(nt + 1) * 512], in_=PS[b][64:128, :]),
                             r=[PSR[b]], w=[RM("qt1_%d" % nt)])
                    else:
                        S.op("dve", lambda e, b=b, nt=nt: e.tensor_copy(out=KT[:, nt * 512:(nt + 1) * 512], in_=PS[b][:]),
                             r=[PSR[b]], w=[RM("kt_%d" % nt)])

            def s1_groups(u, st):
                qb, m = u
                QTm = QT0 if m == 0 else QT1
                qres = RM("qt%d_%d" % (m, qb // 2))
                groups = []
                for p in range(qb + 1):
                    def g(p=p):
                        b = ST_BANKS[ring("st", 3)]
                        for j in range(2):
                            kt = 2 * p + j
                            jd = kt - 2 * qb
                            S.op("pe", lambda e, kt=kt, j=j, jd=jd: e.matmul(PS[b][:, j * 256:(j + 1) * 256], lhsT=KT[:, kt * 128:(kt + 1) * 128],
                                                                             rhs=QTm[:, qb * 256:(qb + 1) * 256], start=True, stop=(jd < 0)),
                                 r=[RM("kt_%d" % (kt // 4)), qres], w=[PSR[b]])
                            if jd >= 0:
                                S.op("pe", lambda e, j=j, jd=jd: e.matmul(PS[b][:, j * 256:(j + 1) * 256], lhsT=MASKA[:], rhs=MASKB[:, jd, :],
                                                                          start=False, stop=True),
                                     r=[RM("maska"), RM("maskb")], w=[PSR[b]])
                        S.op("act", lambda e: e.activation(out=PT[:, st, 2 * p:2 * p + 2, :], in_=PS[b][:].rearrange("p (j x) -> p j x", j=2),
                                                           func=AF.Exp, scale=0.125),
                             r=[PSR[b]], w=[RM("pt%d_%d" % (st, p))])
                    groups.append(g)
                return groups

            def s2_list(u, st):
                qb, m = u
                fl = []
                for jq in range(2):
                    last_kt = 2 * qb + jq
                    for kt in range(last_kt + 1):
                        def f(jq=jq, kt=kt, last_kt=last_kt):
                            ab = 4 * m + (qb % 2)
                            S.op("pe", lambda e: e.matmul(
                                PS[ab][:, jq * 256:jq * 256 + 129], lhsT=PT[:, st, kt, jq * 128:(jq + 1) * 128], rhs=V1[:, kt, h, 0:129],
                                start=(kt == 0), stop=(kt == last_kt)),
                                r=[RM("pt%d_%d" % (st, kt // 2)), RM("v1_%d" % kt), RM("v1ones")], w=[PSR[ab]])
                        fl.append(f)
                return fl

            def epilogue(qb):
                acc0 = PSALL[:, 0:1024].rearrange("p (j x) -> p j x", j=4)
                acc1 = PSALL[:, 2048:3072].rearrange("p (j x) -> p j x", j=4)
                a0r, a1r = [PSR[0], PSR[1]], [PSR[4], PSR[5]]
                bc = lambda ap: ap.unsqueeze(2).to_broadcast([128, 4, 128])
                S.op("dve", lambda e: e.reciprocal(out=SCR[:, 640:644], in_=acc0[:, :, 128]), r=a0r, w=[RM("ep_r1")])
                S.op("dve", lambda e: e.reciprocal(out=SCR[:, 644:648], in_=acc1[:, :, 128]), r=a1r, w=[RM("ep_r2")])
                S.op("dve", lambda e: e.tensor_scalar(out=SCR[:, 644:648], in0=SCR[:, 644:648], scalar1=LAMV[:, l:l + 1], scalar2=None, op0=ALU.mult),
                     r=[RM("ep_r2"), RM("lamv")], w=[RM("ep_r2")])
                S.op("dve", lambda e: e.tensor_tensor(out=OST[:, 0, :, :], in0=acc1[:, :, 0:128], in1=bc(SCR[:, 644:648]), op=ALU.mult),
                     r=a1r + [RM("ep_r2")], w=[RM("ost0")])
                S.op("dve", lambda e: e.tensor_tensor(out=OST[:, 1, :, :], in0=acc0[:, :, 0:128], in1=bc(SCR[:, 640:644]), op=ALU.mult),
                     r=a0r + [RM("ep_r1")], w=[RM("ost1")])
                S.op("dve", lambda e: e.tensor_tensor(out=OST[:, 1, :, :], in0=OST[:, 1, :, :], in1=OST[:, 0, :, :], op=ALU.subtract),
                     r=[RM("ost0"), RM("ost1")], w=[RM("ost1")])
                S.op("dve", lambda e: e.tensor_tensor(out=OST[:, 0, :, :], in0=OST[:, 1, :, :], in1=OST[:, 1, :, :], op=ALU.mult),
                     r=[RM("ost1")], w=[RM("ost0")])
                S.op("dve", lambda e: e.reduce_sum(out=SCR[:, 648:652], in_=OST[:, 0, :, :], axis=AX.X), r=[RM("ost0")], w=[RM("ep_ss")])
                S.op("act", lambda e: e.activation(out=SCR[:, 652:656], in_=SCR[:, 648:652], func=AF.Sqrt, bias=HN_EPS, scale=1.0 / 128),
                     r=[RM("ep_ss")], w=[RM("ep_rms")])
                S.op("dve", lambda e: e.reciprocal(out=SCR[:, 656:660], in_=SCR[:, 652:656]), r=[RM("ep_rms")], w=[RM("ep_ri")])
                S.op("dve", lambda e: e.tensor_tensor(out=OST[:, 1, :, :], in0=OST[:, 1, :, :], in1=bc(SCR[:, 656:660]), op=ALU.mult),
                     r=[RM("ost1"), RM("ep_ri")], w=[RM("ost1")])
                S.op("dve", lambda e: e.tensor_tensor(out=OST[:, 0, :, :], in0=OST[:, 1, :, :],
                                                      in1=SUBG[:, l, :].unsqueeze(1).to_broadcast([128, 4, 128]), op=ALU.mult),
                     r=[RM("ost1"), RM("subg")], w=[RM("ost0")])
                for jq in range(4):
                    S.op("pe", lambda e, jq=jq: e.transpose(out=PS[6][:, jq * 128:(jq + 1) * 128], in_=OST[:, 0, jq, :], identity=IDENT[:]),
                         r=[RM("ost0"), RM("ident")], w=[PSR[6]])
                S.op("act", lambda e: e.activation(out=YT[:, h, (qb - 1) * 256:(qb + 1) * 256], in_=PS[6][:], func=AF.Copy),
                     r=[PSR[6]], w=[R_yt[h][qb // 2]])

            units = [(qb, m) for qb in range(8) for m in range(2)]
            for g in s1_groups(units[0], 0):
                g()
            for i, u in enumerate(units):
                nxt = s1_groups(units[i + 1], (i + 1) % 2) if i + 1 < len(units) else []
                pv = s2_list(u, i % 2)
                if nxt:
                    per = -(-len(pv) // len(nxt))
                    idx = 0
                    for g in nxt:
                        g()
                        for f in pv[idx:idx + per]:
                            f()
                        idx += per
                    for f in pv[idx:]:
                        f()
                else:
                    for f in pv:
                        f()
                if u[1] == 1 and u[0] % 2 == 1:
                    epilogue(u[0])

        S.barrier()
        S.op("pool", lambda e, l=l: e.dma_start(out=WO[:], in_=w_out_d[l].rearrange("(k p) n -> p k n", p=128)), w=[RM("wo")], dma_sem="wo")
        if is_moe:
            S.op("sp", lambda e, l=l: e.dma_start(out=WR[:], in_=rt_d[l // 2].rearrange("(k p) n -> p k n", p=128)), w=[RM("wr")], dma_sem="wr")
            S.op("dve", lambda e: e.tensor_copy(out=WRH[:], in_=WR[:]), r=[RM("wr")], w=[RM("wrh")])
            S.op("dve", lambda e: e.tensor_tensor(out=WRL[:], in0=WR[:], in1=WRH[:], op=ALU.subtract), r=[RM("wr"), RM("wrh")], w=[RM("wrhl")])

        def layer_norm_group(tiles, ga, ba, gp):
            q0 = 700 + 80 * gp
            pr = "ln%d_" % gp
            for j, tt in enumerate(tiles):
                S.op("dve", lambda e, j=j, tt=tt: e.bn_stats(out=SCR[:, q0 + 12 * j:q0 + 12 * j + 6], in_=XRES[:, tt, 0:512]),
                     r=[R_xres[tt]], w=[RM(pr + "s%da" % j)])
                S.op("dve", lambda e, j=j, tt=tt: e.bn_stats(out=SCR[:, q0 + 12 * j + 6:q0 + 12 * j + 12], in_=XRES[:, tt, 512:1024]),
                     r=[R_xres[tt]], w=[RM(pr + "s%db" % j)])
            for j, tt in enumerate(tiles):
                S.op("dve", lambda e, j=j: e.bn_aggr(out=SCR[:, q0 + 48 + 2 * j:q0 + 50 + 2 * j], in_=SCR[:, q0 + 12 * j:q0 + 12 * j + 12]),
                     r=[RM(pr + "s%da" % j), RM(pr + "s%db" % j)], w=[RM(pr + "mv")])
            mv = SCR[:, q0 + 48:q0 + 56].rearrange("p (j t) -> p j t", t=2)
            S.op("act", lambda e: e.activation(out=SCR[:, q0 + 56:q0 + 60], in_=mv[:, :, 1], func=AF.Sqrt, bias=LN_EPS), r=[RM(pr + "mv")], w=[RM(pr + "sd")])
            S.op("dve", lambda e: e.reciprocal(out=SCR[:, q0 + 60:q0 + 64], in_=SCR[:, q0 + 56:q0 + 60]), r=[RM(pr + "sd")], w=[RM(pr + "rs")])
            S.op("dve", lambda e: e.scalar_tensor_tensor(out=SCR[:, q0 + 64:q0 + 68], in0=mv[:, :, 0], scalar=-1.0, in1=SCR[:, q0 + 60:q0 + 64],
                                                         op0=ALU.mult, op1=ALU.mult), r=[RM(pr + "mv"), RM(pr + "rs")], w=[RM(pr + "nm")])
            for j, tt in enumerate(tiles):
                S.op("act", lambda e, j=j, tt=tt: e.activation(out=XRES[:, tt, :], in_=XRES[:, tt, :], func=AF.Identity,
                                                               bias=SCR[:, q0 + 64 + j:q0 + 65 + j], scale=SCR[:, q0 + 60 + j:q0 + 61 + j]),
                     r=[R_xres[tt], RM(pr + "rs"), RM(pr + "nm")], w=[R_xres[tt]])
                S.op("dve", lambda e, tt=tt: e.tensor_tensor(out=XRES[:, tt, :], in0=XRES[:, tt, :], in1=LNBC[:, ga, :], op=ALU.mult),
                     r=[R_xres[tt], RM("lnbc")], w=[R_xres[tt]])
                S.op("dve", lambda e, tt=tt: e.tensor_tensor(out=XRES[:, tt, :], in0=XRES[:, tt, :], in1=LNBC[:, ba, :], op=ALU.add),
                     r=[R_xres[tt], RM("lnbc")], w=[R_xres[tt]])

        LALL = SCR[:, 0:128].rearrange("p (t e) -> p t e", e=NE)
        last_wout = [None]

        def wout_group(g):
            for tt in range(4 * g, 4 * g + 4):
                for hf in range(2):
                    b = ring("po", 4)
                    for k in range(8):
                        last_wout[0] = S.op("pe", lambda e, b=b, k=k, hf=hf, tt=tt: e.matmul(PS[b][:], lhsT=YT[:, k, tt * 128:(tt + 1) * 128], rhs=WO[:, k, hf * 512:(hf + 1) * 512],
                                                                          start=(k == 0), stop=(k == 7)),
                             r=[RM("wo"), R_yt[k][tt // 4]], w=[PSR[b]])
                    if l == 0:
                        S.op("dve", lambda e, b=b, hf=hf, tt=tt: e.scalar_tensor_tensor(
                            out=XRES[:, tt, hf * 512:(hf + 1) * 512], in0=XRES[:, tt, hf * 512:(hf + 1) * 512], scalar=ALPHA,
                            in1=PS[b][:], op0=ALU.mult, op1=ALU.add),
                            r=[PSR[b], R_xres[tt]], w=[R_xres[tt]])
                    else:
                        S.op("dve", lambda e, b=b, hf=hf, tt=tt: e.tensor_tensor(out=XRES[:, tt, hf * 512:(hf + 1) * 512], in0=PS[b][:],
                                                                                 in1=XRES[:, tt, hf * 512:(hf + 1) * 512], op=ALU.add),
                             r=[PSR[b], R_xres[tt]], w=[R_xres[tt]])

        for g in range(4):
            wout_group(g)
        for g in range(4):
            tiles = list(range(4 * g, 4 * g + 4))
            layer_norm_group(tiles, 0, 1, g % 2)
            for tt in tiles:
                b = (4 + ring("tr", 4)) if is_moe else None
                emit_transposes(tt, router_bank=b)
                if is_moe:
                    S.op("act", lambda e, b=b, tt=tt: e.activation(out=LALL[:, tt, :], in_=PS[b][:, 0:NE], func=AF.Copy), r=[PSR[b]], w=[RM("rt_l")])
        if is_moe:
            bc8 = lambda ap: ap.unsqueeze(2).to_broadcast([128, NT, NE])
            v3 = lambda a, b_: SCR[:, a:b_].rearrange("p (t e) -> p t e", e=NE)
            M0, EQ, M1, MK, EX, SM, RD = SCR[:, 128:144], v3(144, 272), SCR[:, 272:288], v3(288, 416), v3(416, 544), SCR[:, 544:560], SCR[:, 560:576]
            S.op("dve", lambda e: e.tensor_reduce(out=M0, in_=LALL, axis=AX.X, op=ALU.max), r=[RM("rt_l")], w=[RM("rt_m0")])
            S.op("dve", lambda e: e.tensor_tensor(out=EQ, in0=LALL, in1=bc8(M0), op=ALU.is_equal), r=[RM("rt_l"), RM("rt_m0")], w=[RM("rt_eq")])
            S.op("dve", lambda e: e.scalar_tensor_tensor(out=EQ, in0=EQ, scalar=-1e30, in1=LALL, op0=ALU.mult, op1=ALU.add),
                 r=[RM("rt_eq"), RM("rt_l")], w=[RM("rt_eq")])
            S.op("dve", lambda e: e.tensor_reduce(out=M1, in_=EQ, axis=AX.X, op=ALU.max), r=[RM("rt_eq")], w=[RM("rt_m1")])
            S.op("dve", lambda e: e.tensor_tensor(out=MK, in0=LALL, in1=bc8(M1), op=ALU.is_ge), r=[RM("rt_l"), RM("rt_m1")], w=[RM("rt_mk")])
            S.op("dve", lambda e: e.tensor_tensor(out=EX, in0=LALL, in1=bc8(M0), op=ALU.subtract), r=[RM("rt_l"), RM("rt_m0")], w=[RM("rt_ex")])
            S.op("act", lambda e: e.activation(out=EX, in_=EX, func=AF.Exp), r=[RM("rt_ex")], w=[RM("rt_ex")])
            S.op("dve", lambda e: e.tensor_tensor(out=EX, in0=EX, in1=MK, op=ALU.mult), r=[RM("rt_ex"), RM("rt_mk")], w=[RM("rt_ex")])
            S.op("dve", lambda e: e.reduce_sum(out=SM, in_=EX, axis=AX.X), r=[RM("rt_ex")], w=[RM("rt_sm")])
            S.op("dve", lambda e: e.reciprocal(out=RD, in_=SM), r=[RM("rt_sm")], w=[RM("rt_rd")])
            S.op("dve", lambda e: e.tensor_tensor(out=GW[:, :, :], in0=EX, in1=bc8(RD), op=ALU.mult), r=[RM("rt_ex"), RM("rt_rd")],
                 w=[RM("gw%d" % tt) for tt in range(NT)])

        experts = list(range(_DBG.get('n_exp', NE))) if is_moe else [0]
        for ex in experts:
            if is_moe:
                wg_src, wu_src, wd_src = mg_d[l // 2][ex], mu_d[l // 2][ex], md_d[l // 2][ex]
            else:
                wg_src, wu_src, wd_src = fg_d[l // 2], fu_d[l // 2], fd_d[l // 2]
            for (c0, ncg) in GROUPS:
                sl = ring("ffw", 2)
                ncol = ncg * 128
                S.op("pool", lambda e, sl=sl, c0=c0, ncol=ncol, src=wg_src: e.dma_start(
                    out=WG[:, sl, :, 0:ncol], in_=src[:, c0 * 128:c0 * 128 + ncol].rearrange("(k p) n -> p k n", p=128)),
                    w=[RM("wg%d" % sl)], dma_sem="wg%d" % sl, extra_deps=[last_wout[0]])
                S.op("pool", lambda e, sl=sl, c0=c0, ncol=ncol, src=wu_src: e.dma_start(
                    out=WU[:, sl, :, 0:ncol], in_=src[:, c0 * 128:c0 * 128 + ncol].rearrange("(k p) n -> p k n", p=128)),
                    w=[RM("wu%d" % sl)], dma_sem="wu%d" % sl, extra_deps=[last_wout[0]])
                S.op("pool", lambda e, sl=sl, c0=c0, ncg=ncg, src=wd_src: e.dma_start(
                    out=WD[:, sl, 0:ncg, :], in_=src[c0 * 128:(c0 + ncg) * 128, :].rearrange("(c p) n -> p c n", p=128)),
                    w=[RM("wd%d" % sl)], dma_sem="wd%d" % sl, extra_deps=[last_wout[0]])
                for cc in range(ncg):
                    for nt in range(4):
                        bg = ring("fg", 2)
                        bu = 2 + ring("fu", 2)
                        for k in range(8):
                            S.op("pe", lambda e, bg=bg, k=k, cc=cc, nt=nt, sl=sl: e.matmul(
                                PS[bg][:], lhsT=WG[:, sl, k, cc * 128:(cc + 1) * 128], rhs=XT[:, k, nt * 512:(nt + 1) * 512], start=(k == 0), stop=(k == 7)),
                                r=[RM("wg%d" % sl)] + xt_blk(4 * nt, 4 * nt + 4), w=[PSR[bg]])
                        for k in range(8):
                            S.op("pe", lambda e, bu=bu, k=k, cc=cc, nt=nt, sl=sl: e.matmul(
                                PS[bu][:], lhsT=WU[:, sl, k, cc * 128:(cc + 1) * 128], rhs=XT[:, k, nt * 512:(nt + 1) * 512], start=(k == 0), stop=(k == 7)),
                                r=[RM("wu%d" % sl)] + xt_blk(4 * nt, 4 * nt + 4), w=[PSR[bu]])
                        ss = ring("sil", 2)
                        S.op("act", lambda e, bg=bg, ss=ss: e.activation(out=SIL[:, ss, :], in_=PS[bg][:], func=AF.Silu), r=[PSR[bg]], w=[RM("sil%d" % ss)])
                        S.op("dve", lambda e, bu=bu, ss=ss, cc=cc, nt=nt, sl=sl: e.tensor_tensor(
                            out=ACTT[:, sl, cc, nt * 512:(nt + 1) * 512], in0=PS[bu][:], in1=SIL[:, ss, :], op=ALU.mult),
                            r=[PSR[bu], RM("sil%d" % ss)], w=[RM("actt%d_%d_%d" % (sl, cc, nt))])
                for tt in range(NT):
                    b0 = 4 + 2 * ring("fd", 2)
                    for cc in range(ncg):
                        for hf in range(2):
                            S.op("pe", lambda e, b0=b0, cc=cc, hf=hf, tt=tt, sl=sl, ncg=ncg: e.matmul(
                                PS[b0 + hf][:], lhsT=ACTT[:, sl, cc, tt * 128:(tt + 1) * 128], rhs=WD[:, sl, cc, hf * 512:(hf + 1) * 512],
                                start=(cc == 0), stop=(cc == ncg - 1)),
                                r=[RM("wd%d" % sl), RM("actt%d_%d_%d" % (sl, cc, tt // 4))], w=[PSR[b0 + hf]])
                    for hf in range(2):
                        if is_moe:
                            S.op("dve", lambda e, b0=b0, hf=hf, tt=tt, ex=ex: e.scalar_tensor_tensor(
                                out=XRES[:, tt, hf * 512:(hf + 1) * 512], in0=PS[b0 + hf][:], scalar=GW[:, tt, ex:ex + 1],
                                in1=XRES[:, tt, hf * 512:(hf + 1) * 512], op0=ALU.mult, op1=ALU.add),
                                r=[PSR[b0 + hf], R_xres[tt], RM("gw%d" % tt)], w=[R_xres[tt]])
                        else:
                            S.op("dve", lambda e, b0=b0, hf=hf, tt=tt: e.tensor_tensor(
                                out=XRES[:, tt, hf * 512:(hf + 1) * 512], in0=PS[b0 + hf][:], in1=XRES[:, tt, hf * 512:(hf + 1) * 512], op=ALU.add),
                                r=[PSR[b0 + hf], R_xres[tt]], w=[R_xres[tt]])

        for g in range(4):
            tiles = list(range(4 * g, 4 * g + 4))
            layer_norm_group(tiles, 2, 3, g % 2)
            if not last:
                for tt in tiles:
                    emit_transposes(tt)

    for q in range(4):
        o = S.op("sp", lambda e, q=q: e.dma_start(
            out=out_d[512 * q:512 * (q + 1), :].rearrange("(t p) d -> p t d", p=128), in_=XRES[:, 4 * q:4 * q + 4, :]),
            r=R_xres[4 * q:4 * q + 4], dma_sem="out%d" % q)
        S.final.append(o)
    S.emit(nc)
    return nc


def _prep_shared(inp):
    f = lambda a: np.ascontiguousarray(np.asarray(a, dtype=np.float32))
    bdw = np.zeros((DEPTH, 6, 128, 128), np.float32)
    for l in range(DEPTH):
        for si, key in enumerate(("pool_w", "lru_wa", "lru_wi")):
            w = np.asarray(inp[key][l], np.float32)
            for c in range(2):
                for hf in range(2):
                    bdw[l, 2 * si + c, 64 * hf:64 * hf + 64, 64 * hf:64 * hf + 64] = w[2 * c + hf]
    pp = np.zeros((128, DEPTH, 18), np.float32)
    for l in range(DEPTH):
        for c in range(2):
            sl = slice(128 * c, 128 * c + 128)
            for j in range(4):
                pp[:, l, c * 9 + j] = np.asarray(inp["conv_w"][l][j][sl])
            pp[:, l, c * 9 + 4] = np.asarray(inp["conv_b"][l][sl])
            pp[:, l, c * 9 + 5] = np.asarray(inp["lru_ba"][l][sl])
            pp[:, l, c * 9 + 6] = np.asarray(inp["lru_bi"][l][sl])
            pp[:, l, c * 9 + 7] = np.asarray(inp["lru_lambda"][l][sl])
            pp[:, l, c * 9 + 8] = np.asarray(inp["pool_scale"][l][sl])
    lnp = np.stack([np.asarray(inp[k], np.float32) for k in ("ln1_g", "ln1_b", "ln2_g", "ln2_b")], axis=1)
    shared = {
        "w_in": f(inp["w_in"]), "w_out": f(inp["w_out"]),
        "attn_lambda": f(inp["attn_lambda"]).reshape(-1), "attn_subln_g": f(inp["attn_subln_g"]).reshape(-1),
        "bdw": bdw, "pp": pp, "lnp": f(lnp),
        "ffn_w_gate": f(inp["ffn_w_gate"]), "ffn_w_up": f(inp["ffn_w_up"]), "ffn_w_down": f(inp["ffn_w_down"]),
        "router_w": f(inp["router_w"]), "moe_w_gate": f(inp["moe_w_gate"]), "moe_w_up": f(inp["moe_w_up"]),
        "moe_w_down": f(inp["moe_w_down"]), "ident": np.eye(128, dtype=np.float32),
    }
    return shared


def kernel(**inputs):
    x = np.asarray(inputs["x"], np.float32)
    shared = _prep_shared(inputs)
    nc = build_program(DEPTH)
    in_maps = []
    for b in range(8):
        m = dict(shared)
        m["x"] = np.ascontiguousarray(x[b])
        in_maps.append(m)
    res = run_bass_kernel_spmd(nc, in_maps, core_ids=list(range(8)))
    return np.stack([np.asarray(r["out"], np.float32) for r in res.results], axis=0)
```
